# Optimizing a Trainium2 kernel written in Bass

```python
import jax, jax.numpy as jnp
from jax import lax
import numpy as np

D_MODEL = 2048
BATCH = 4
SEQ = 4096
DEPTH = 1
DEC_BATCH = 16
DEC_SEQ = 32
PAST_LEN = 1024

CHUNK = 64
Q_BLOCK = 128
PLE_DIM = 256
SB_HEADS = 8
SB_HEAD_DIM = 128
SB_WIDTH = SB_HEADS * SB_HEAD_DIM
MLA_HEADS = 8
MLA_NOPE_DIM = 128
MLA_ROPE_DIM = 64
MLA_V_DIM = 128
MLA_Q_LORA = 512
MLA_KV_LORA = 512
BRANCH_WIDTH = 1024
N_BRANCH = 2
D_FF = 4 * D_MODEL
IN_WIDTH = 3 * SB_WIDTH + MLA_Q_LORA + MLA_KV_LORA + MLA_ROPE_DIM + N_BRANCH * D_MODEL
ROPE_THETA = 10000.0
EPS = 1e-6
SB_SCALE = SB_HEAD_DIM ** -0.5
MLA_SCALE = (MLA_NOPE_DIM + MLA_ROPE_DIM) ** -0.5

kernel_name = "stickbreak_mla_parallel_streaming_step"


def rmsnorm(x, g):
    xf = x.astype(jnp.float32)
    y = xf * lax.rsqrt(jnp.mean(xf * xf, axis=-1, keepdims=True) + EPS)
    return (y * g.astype(jnp.float32)).astype(x.dtype)


def rope(x, pos):
    half = x.shape[-1] // 2
    freqs = ROPE_THETA ** (-jnp.arange(half, dtype=jnp.float32) / half)
    ang = pos.astype(jnp.float32)[:, None] * freqs[None, :]
    ang = ang.reshape((1, pos.shape[0]) + (1,) * (x.ndim - 3) + (half,))
    cos, sin = jnp.cos(ang), jnp.sin(ang)
    xf = x.astype(jnp.float32)
    x1, x2 = xf[..., :half], xf[..., half:]
    return jnp.concatenate([x1 * cos - x2 * sin, x1 * sin + x2 * cos], axis=-1).astype(x.dtype)


def stick_breaking(q, k, v, q_pos, k_pos):
    z = jnp.einsum('bqhd,bkhd->bhqk', q, k).astype(jnp.float32) * SB_SCALE
    mask = k_pos[None, :] < q_pos[:, None]
    log_beta = jax.nn.log_sigmoid(z)
    log_1mb = jnp.where(mask, log_beta - z, 0.0)
    after = lax.cumsum(log_1mb, axis=3, reverse=True) - log_1mb
    w = jnp.where(mask, jnp.exp(log_beta + after), 0.0)
    return jnp.einsum('bhqk,bkhd->bqhd', w.astype(v.dtype), v)


def mla_attend(q_nope, q_rope, k_nope, k_rope, v, q_pos, k_pos):
    s = (jnp.einsum('bqhd,bkhd->bhqk', q_nope, k_nope)
         + jnp.einsum('bqhr,bkr->bhqk', q_rope, k_rope)).astype(jnp.float32) * MLA_SCALE
    mask = (k_pos // CHUNK)[None, :] <= (q_pos // CHUNK)[:, None]
    p = jax.nn.softmax(jnp.where(mask, s, -jnp.inf), axis=-1)
    return jnp.einsum('bhqk,bkhd->bqhd', p.astype(v.dtype), v)


def layer(x, ple, past, pos0, lw):
    b, n, _ = x.shape
    q_pos = pos0 + jnp.arange(n)
    h = rmsnorm(x, lw['g_mix_pre'])
    proj = h @ lw['w_in']
    sizes = (SB_WIDTH, SB_WIDTH, SB_WIDTH, MLA_Q_LORA, MLA_KV_LORA, MLA_ROPE_DIM, N_BRANCH * D_MODEL)
    points = np.cumsum(sizes)[:-1].tolist()
    sb_q, sb_k, sb_v, c_q, c_kv, k_rope, gate_logits = jnp.split(proj, points, axis=-1)
    sb_q = sb_q.reshape(b, n, SB_HEADS, SB_HEAD_DIM)
    sb_k = sb_k.reshape(b, n, SB_HEADS, SB_HEAD_DIM)
    sb_v = sb_v.reshape(b, n, SB_HEADS, SB_HEAD_DIM)
    q = (rmsnorm(c_q, lw['g_q']) @ lw['w_uq']).reshape(b, n, MLA_HEADS, MLA_NOPE_DIM + MLA_ROPE_DIM)
    q_nope, q_rope = q[..., :MLA_NOPE_DIM], rope(q[..., MLA_NOPE_DIM:], q_pos)
    c_kv = rmsnorm(c_kv, lw['g_kv'])
    k_rope = rope(k_rope, q_pos)
    new_state = (sb_k, sb_v, c_kv, k_rope)
    if past is not None:
        sb_k = jnp.concatenate([past[0], sb_k], axis=1)
        sb_v = jnp.concatenate([past[1], sb_v], axis=1)
        c_kv = jnp.concatenate([past[2], c_kv], axis=1)
        k_rope = jnp.concatenate([past[3], k_rope], axis=1)
    total = sb_k.shape[1]
    past_len = total - n
    k_pos = jnp.arange(total)
    k_nope = (c_kv @ lw['w_uk']).reshape(b, total, MLA_HEADS, MLA_NOPE_DIM)
    v_mla = (c_kv @ lw['w_uv']).reshape(b, total, MLA_HEADS, MLA_V_DIM)
    o_sb, o_mla = [], []
    for i0 in range(0, n, Q_BLOCK):
        i1 = min(i0 + Q_BLOCK, n)
        ke = past_len + i1
        qp, kp = q_pos[i0:i1], k_pos[:ke]
        o_sb.append(stick_breaking(sb_q[:, i0:i1], sb_k[:, :ke], sb_v[:, :ke], qp, kp))
        o_mla.append(mla_attend(q_nope[:, i0:i1], q_rope[:, i0:i1], k_nope[:, :ke],
                                k_rope[:, :ke], v_mla[:, :ke], qp, kp))
    o_sb = jnp.concatenate(o_sb, axis=1).reshape(b, n, BRANCH_WIDTH)
    o_mla = jnp.concatenate(o_mla, axis=1).reshape(b, n, BRANCH_WIDTH)
    branches = jnp.stack([o_sb, o_mla], axis=2)
    gates = jax.nn.sigmoid(gate_logits.reshape(b, n, N_BRANCH, D_MODEL))
    merged = jnp.sum(gates * jnp.einsum('bnkc,kcd->bnkd', branches, lw['w_branch']), axis=2)
    x = x + rmsnorm(merged @ lw['w_out'], lw['g_mix_post'])
    f = jnp.square(jax.nn.relu(rmsnorm(x, lw['g_ffn_pre']) @ lw['w_up'])) @ lw['w_down']
    x = x + rmsnorm(f, lw['g_ffn_post'])
    pg = jax.nn.sigmoid(rmsnorm(x, lw['g_ple_gate']) @ lw['w_ple_gate'])
    x = x + rmsnorm((ple @ lw['w_ple']) * pg, lw['g_ple_post'])
    return x, new_state


def setup_inputs(seed: int = 0) -> dict:
    key = jax.random.key(seed)
    ks = jax.random.split(key, 32)
    f32 = jnp.float32

    def nrm(k, shape, fan_in):
        return jax.random.normal(k, shape, f32) * (fan_in ** -0.5)

    def gain(k, dim):
        return 1.0 + 0.05 * jax.random.normal(k, (DEPTH, dim), f32)

    return {
        'x_prompt': jax.random.normal(ks[0], (BATCH, SEQ, D_MODEL), f32),
        'x_sample': jax.random.normal(ks[1], (DEC_BATCH, DEC_SEQ, D_MODEL), f32),
        'cache_sb_k': jax.random.normal(ks[2], (DEPTH, DEC_BATCH, PAST_LEN, SB_HEADS, SB_HEAD_DIM), f32),
        'cache_sb_v': jax.random.normal(ks[3], (DEPTH, DEC_BATCH, PAST_LEN, SB_HEADS, SB_HEAD_DIM), f32),
        'cache_mla_ckv': jax.random.normal(ks[4], (DEPTH, DEC_BATCH, PAST_LEN, MLA_KV_LORA), f32),
        'cache_mla_krope': jax.random.normal(ks[5], (DEPTH, DEC_BATCH, PAST_LEN, MLA_ROPE_DIM), f32),
        'p_prompt': jax.random.normal(ks[6], (DEPTH, BATCH, SEQ, PLE_DIM), f32),
        'p_sample': jax.random.normal(ks[7], (DEPTH, DEC_BATCH, DEC_SEQ, PLE_DIM), f32),
        'g_mix_pre': gain(ks[8], D_MODEL),
        'w_in': nrm(ks[9], (DEPTH, D_MODEL, IN_WIDTH), D_MODEL),
        'g_q': gain(ks[10], MLA_Q_LORA),
        'w_uq': nrm(ks[11], (DEPTH, MLA_Q_LORA, MLA_HEADS * (MLA_NOPE_DIM + MLA_ROPE_DIM)), MLA_Q_LORA),
        'g_kv': gain(ks[12], MLA_KV_LORA),
        'w_uk': nrm(ks[13], (DEPTH, MLA_KV_LORA, MLA_HEADS * MLA_NOPE_DIM), MLA_KV_LORA),
        'w_uv': nrm(ks[14], (DEPTH, MLA_KV_LORA, MLA_HEADS * MLA_V_DIM), MLA_KV_LORA),
        'w_branch': nrm(ks[15], (DEPTH, N_BRANCH, BRANCH_WIDTH, D_MODEL), BRANCH_WIDTH),
        'w_out': nrm(ks[16], (DEPTH, D_MODEL, D_MODEL), D_MODEL),
        'g_mix_post': gain(ks[17], D_MODEL),
        'g_ffn_pre': gain(ks[18], D_MODEL),
        'w_up': nrm(ks[19], (DEPTH, D_MODEL, D_FF), D_MODEL),
        'w_down': nrm(ks[20], (DEPTH, D_FF, D_MODEL), D_FF),
        'g_ffn_post': gain(ks[21], D_MODEL),
        'g_ple_gate': gain(ks[22], D_MODEL),
        'w_ple_gate': nrm(ks[23], (DEPTH, D_MODEL, D_MODEL), D_MODEL),
        'w_ple': nrm(ks[24], (DEPTH, PLE_DIM, D_MODEL), PLE_DIM),
        'g_ple_post': gain(ks[25], D_MODEL),
    }


def reference(x_prompt, x_sample, cache_sb_k, cache_sb_v, cache_mla_ckv, cache_mla_krope,
              p_prompt, p_sample, g_mix_pre, w_in, g_q, w_uq, g_kv, w_uk, w_uv, w_branch,
              w_out, g_mix_post, g_ffn_pre, w_up, w_down, g_ffn_post, g_ple_gate,
              w_ple_gate, w_ple, g_ple_post):
    past_len = cache_sb_k.shape[2]
    yp, ys = x_prompt, x_sample
    st_p, st_s = [], []
    for i in range(DEPTH):
        lw = {'g_mix_pre': g_mix_pre[i], 'w_in': w_in[i], 'g_q': g_q[i], 'w_uq': w_uq[i],
              'g_kv': g_kv[i], 'w_uk': w_uk[i], 'w_uv': w_uv[i], 'w_branch': w_branch[i],
              'w_out': w_out[i], 'g_mix_post': g_mix_post[i], 'g_ffn_pre': g_ffn_pre[i],
              'w_up': w_up[i], 'w_down': w_down[i], 'g_ffn_post': g_ffn_post[i],
              'g_ple_gate': g_ple_gate[i], 'w_ple_gate': w_ple_gate[i], 'w_ple': w_ple[i],
              'g_ple_post': g_ple_post[i]}
        yp, sp = layer(yp, p_prompt[i], None, 0, lw)
        past = (cache_sb_k[i], cache_sb_v[i], cache_mla_ckv[i], cache_mla_krope[i])
        ys, ss = layer(ys, p_sample[i], past, past_len, lw)
        st_p.append(sp)
        st_s.append(ss)
    sb_k_p = jnp.stack([s[0] for s in st_p])
    sb_v_p = jnp.stack([s[1] for s in st_p])
    ckv_p = jnp.stack([s[2] for s in st_p])
    kr_p = jnp.stack([s[3] for s in st_p])
    sb_k_s = jnp.stack([s[0] for s in st_s])
    sb_v_s = jnp.stack([s[1] for s in st_s])
    ckv_s = jnp.stack([s[2] for s in st_s])
    kr_s = jnp.stack([s[3] for s in st_s])
    return (yp, ys, sb_k_p, sb_v_p, ckv_p, kr_p, sb_k_s, sb_v_s, ckv_s, kr_s)
```

```python
import numpy as np
from contextlib import ExitStack
import concourse.bass as bass
import concourse.mybir as mybir
from concourse.bass_utils import run_bass_kernel_spmd

F32 = mybir.dt.float32
BF16 = mybir.dt.bfloat16
AF = mybir.ActivationFunctionType
ALU = mybir.AluOpType

D = 2048
SEQ = 4096
NT = 512
PAST = 1024
DSEQ = 32
NH = 8
EPS = 1e-6
SB_SCALE = 128 ** -0.5
MLA_SCALE = 192 ** -0.5
OWN = {0: [0, 3, 4, 7], 1: [1, 2, 5, 6]}
TMAX = [1, 3, 5, 7]
MASKV = 30000.0
SKEYS = 1152

ENGS = ("sp", "act", "pool", "pe", "dve")
DMA_NS = 24


class Res:
    __slots__ = ("name", "w", "rc", "rd", "excl")

    def __init__(self, name="", excl=False):
        self.name = name
        self.excl = excl
        self.w = None
        self.rc = {}
        self.rd = []


class Op:
    __slots__ = ("eng", "fn", "deps", "is_dma", "need", "sem", "val", "dma_i")

    def __init__(self, eng, fn, is_dma):
        self.eng = eng
        self.fn = fn
        self.deps = ()
        self.is_dma = is_dma
        self.need = False
        self.sem = None
        self.val = None
        self.dma_i = None


class Prog:
    def __init__(self):
        self.ops = {e: [] for e in ENGS}
        self.ndma = {e: 0 for e in ENGS}
        self.planning = False
        self.bar = {}
        self.live_dma = []
        self.last = {}

    def add(self, eng, fn, reads=(), writes=(), dma=False):
        if self.planning:
            return None
        op = Op(eng, fn, dma)
        deps = set()
        for r in reads:
            if r.w is not None:
                deps.add(r.w)
            if r.excl:
                for e2, o2 in r.rc.items():
                    if e2 != eng:
                        deps.add(o2)
        for r in writes:
            if r.w is not None:
                deps.add(r.w)
            deps.update(r.rc.values())
            deps.update(r.rd)
        b = self.bar.pop(eng, None)
        if b:
            deps.update(b)
        op.deps = [d for d in deps if not (d.eng == "pe" and eng == "pe" and not d.is_dma and not dma)]
        for d in op.deps:
            d.need = True
        for r in reads:
            if dma:
                r.rd.append(op)
            else:
                r.rc[eng] = op
        for r in writes:
            r.w = op
            r.rc = {}
            r.rd = []
        if dma:
            op.dma_i = self.ndma[eng]
            self.ndma[eng] += 1
            op.need = True
            self.live_dma.append(op)
        else:
            self.last[eng] = op
        self.ops[eng].append(op)
        return op

    def barrier(self):
        if self.planning:
            return
        deps = list(self.live_dma) + list(self.last.values())
        self.live_dma = []
        for e in ENGS:
            self.bar[e] = list(deps)

    def emit(self, nc, finals):
        engmap = {"sp": "sync", "act": "scalar", "pool": "gpsimd", "pe": "tensor", "dve": "vector"}
        with ExitStack() as es:
            csem = {e: es.enter_context(nc.semaphore("c_" + e)) for e in ENGS}
            dsem = {}
            for e in ENGS:
                if self.ndma[e]:
                    dsem[e] = [es.enter_context(nc.semaphore("d_%s_%d" % (e, i)))
                               for i in range(min(DMA_NS, self.ndma[e]))]
            for e in ENGS:
                cnt = 0
                for op in self.ops[e]:
                    if op.is_dma:
                        op.sem = dsem[e][op.dma_i % DMA_NS]
                        op.val = 16 * (op.dma_i // DMA_NS + 1)
                    elif op.need:
                        cnt += 1
                        op.sem = csem[e]
                        op.val = cnt
            block = es.enter_context(nc.Block())

            def make(e):
                ops = self.ops[e]

                def body(eng):
                    waited = {}
                    for op in ops:
                        ws = {}
                        for d in op.deps:
                            k = id(d.sem)
                            if k not in ws or ws[k][1] < d.val:
                                ws[k] = (d.sem, d.val)
                        if op.is_dma and op.dma_i >= DMA_NS:
                            k = id(op.sem)
                            v = op.val - 16
                            if k not in ws or ws[k][1] < v:
                                ws[k] = (op.sem, v)
                        for k, (s, v) in ws.items():
                            if waited.get(k, 0) >= v:
                                continue
                            waited[k] = v
                            eng.wait_ge(s, v)
                        inst = op.fn(eng)
                        if op.is_dma:
                            inst.then_inc(op.sem, 16)
                        elif op.need:
                            inst.then_inc(op.sem, 1)
                    if e == "sp":
                        ws = {}
                        for op in finals:
                            k = id(op.sem)
                            if k not in ws or ws[k][1] < op.val:
                                ws[k] = (op.sem, op.val)
                        for k, (s, v) in ws.items():
                            if waited.get(k, 0) < v:
                                eng.wait_ge(s, v)
                return body

            for e in ENGS:
                if self.ops[e] or e == "sp":
                    getattr(block, engmap[e])(make(e))


class Builder:
    def __init__(self):
        self.nc = bass.Bass("TRN2", target_bir_lowering=False)
        self.P = Prog()
        self.finals = []
        self.wplan = []
        self.wkeys = {}
        self.wspecs = []
        self.wq = "sp"
        self.wi = 0
        self.wissued = 0
        self.bank_rr = 0

    def din(self, name, shape, dt=F32):
        return self.nc.dram_tensor(name, list(shape), dt, kind="ExternalInput").ap()

    def dout(self, name, shape, dt=F32):
        return self.nc.dram_tensor(name, list(shape), dt, kind="ExternalOutput").ap()

    def dscr(self, name, shape, dt=BF16):
        return self.nc.dram_tensor(name, list(shape), dt, kind="Internal").ap()

    def view(self, off, shape, dt):
        esz = 4 if dt == F32 else 2
        n = int(np.prod(shape))
        assert off % 4 == 0
        nb = n * esz
        assert off + nb <= self.arena_bytes, (off, nb)
        a = self.arena[:, off // 2: (off + nb) // 2]
        if dt == F32:
            a = a.bitcast(F32)
        if len(shape) == 2:
            a = a.rearrange("p (a b) -> p a b", a=shape[0])
        elif len(shape) == 3:
            a = a.rearrange("p (a b c) -> p a b c", a=shape[0], b=shape[1])
        return a

    def mm(self, ps, lhsT, rhs, start, stop, reads, writes):
        self.P.add("pe", lambda e: e.matmul(ps, lhsT=lhsT, rhs=rhs, start=start, stop=stop),
                   reads, writes)

    def tr(self, ps, in_, ident, reads, writes):
        self.P.add("pe", lambda e: e.transpose(ps, in_, ident), reads, writes)

    def act(self, out, in_, func, reads, writes, scale=None, bias=None, accum=None):
        kw = {}
        if scale is not None:
            kw["scale"] = scale
        if bias is not None:
            kw["bias"] = bias
        if accum is not None:
            kw["accum_out"] = accum
        self.P.add("act", lambda e: e.activation(out=out, in_=in_, func=func, **kw), reads, writes)

    def copy(self, eng, out, in_, reads, writes):
        if eng == "act":
            self.P.add("act", lambda e: e.activation(out=out, in_=in_, func=AF.Copy), reads, writes)
        else:
            self.P.add(eng, lambda e: e.tensor_copy(out, in_), reads, writes)

    def ts(self, eng, out, in0, s1, s2, op0, op1, reads, writes):
        if op1 is None:
            self.P.add(eng, lambda e: e.tensor_scalar(out=out, in0=in0, scalar1=s1, scalar2=None, op0=op0),
                       reads, writes)
        else:
            self.P.add(eng, lambda e: e.tensor_scalar(out=out, in0=in0, scalar1=s1, scalar2=s2,
                                                      op0=op0, op1=op1), reads, writes)

    def tt(self, eng, out, in0, in1, op, reads, writes):
        self.P.add(eng, lambda e: e.tensor_tensor(out=out, in0=in0, in1=in1, op=op), reads, writes)

    def stt(self, out, in0, scalar, in1, op0, op1, reads, writes):
        self.P.add("dve", lambda e: e.scalar_tensor_tensor(out=out, in0=in0, scalar=scalar, in1=in1,
                                                           op0=op0, op1=op1), reads, writes)

    def dma(self, eng, out, in_, reads, writes, final=False):
        op = self.P.add(eng, lambda e: e.dma_start(out=out, in_=in_), reads, writes, dma=True)
        if final and op is not None:
            self.finals.append(op)
        return op

    def banks(self, n):
        r = []
        for _ in range(n):
            r.append(self.bank_rr % 8)
            self.bank_rr += 1
        return r

    def wtile(self, src_ap, kcn, ncols):
        key = (src_ap.name, src_ap.offset, str(src_ap.ap))
        if self.P.planning:
            if key not in self.wkeys:
                self.wkeys[key] = len(self.wspecs)
                self.wspecs.append((src_ap, kcn, ncols))
            self.wplan.append((self.wkeys[key], kcn, ncols))
            return self.wt_view[0][:, 0:kcn, 0:ncols], self.R_wt[0]
        i = self.wi
        self.wi += 1
        while self.wissued <= min(i + 1, len(self.wplan) - 1):
            j = self.wissued
            k, k2, n2 = self.wplan[j]
            slot = j % 2
            dst = self.wt_view[slot][:, 0:k2, 0:n2]
            srcb = self.wconv[k].rearrange("p (k n) -> p k n", k=k2)
            if k not in self.wdone:
                uses = self.wuses.get(k, 0)
                self.wuses[k] = uses + 1
                do_wb = (k % 2 == 0) or uses >= 1
                if do_wb:
                    self.wdone.add(k)
                src_ap = self.wspecs[k][0]
                if len(src_ap.shape) == 4:
                    for kc in range(k2):
                        self.dma("pool", dst[:, kc, :].rearrange("p (h e) -> p h e", e=src_ap.shape[3]),
                                 src_ap[:, kc, :, :], [], [self.R_wt[slot]])
                else:
                    self.dma("pool", dst, src_ap, [], [self.R_wt[slot]])
                wb = (srcb, dst, slot, k) if do_wb else None
            else:
                self.dma(self.wq, dst, srcb, [self.R_wconv[k]], [self.R_wt[slot]])
                wb = None
            if self.wb_pending is not None:
                pb_src, pb_dst, pb_slot, pb_k = self.wb_pending
                self.dma("pool", pb_src, pb_dst, [self.R_wt[pb_slot]], [self.R_wconv[pb_k]])
            self.wb_pending = wb
            self.wissued += 1
        slot = i % 2
        return self.wt_view[slot][:, 0:kcn, 0:ncols], self.R_wt[slot]

    def convert_weights(self):
        for k, (src_ap, kcn, ncols) in enumerate(self.wspecs):
            dstb = self.wconv[k].rearrange("p (k n) -> p k n", k=kcn)
            R = self.R_wconv[k]
            Rwin = self.R_cwin[k % 4]
            if len(src_ap.shape) == 4:
                for kc in range(kcn):
                    self.dma("pool", dstb[:, kc, :].rearrange("p (h e) -> p h e", e=src_ap.shape[3]),
                             src_ap[:, kc, :, :], [], [R, Rwin])
            else:
                self.dma("pool", dstb, src_ap, [], [R, Rwin])

    def norm_stats(self, src, rows, n, reads):
        i = self.smi % 4
        self.smi += 1
        sm = self.smr[i]
        R = self.R_smr[i]
        ncol = src.shape[-1]
        self.act(self.junk[0:rows, 0:ncol], src, AF.Square, reads, [R], accum=sm[0:rows, 0:1])
        return self.rstd_tail(sm, R, rows, n)

    def rstd_tail(self, sm, R, rows, n):
        self.ts("dve", sm[0:rows, 1:2], sm[0:rows, 0:1], 1.0 / n, EPS, ALU.mult, ALU.add, [R], [R])
        self.act(sm[0:rows, 1:2], sm[0:rows, 1:2], AF.Sqrt, [R], [R])
        self.P.add("dve", lambda e: e.reciprocal(sm[0:rows, 1:2], sm[0:rows, 1:2]), [R], [R])
        return sm[0:rows, 1:2], R

    def defer(self, fn):
        self.deferred.append(fn)

    def flush(self, keep=0):
        while len(self.deferred) > keep:
            self.deferred.pop(0)()

    def lin_fm(self, wsrc, kcn, ncols_total, rhs, R_rhs, width, cb, wres=None):
        for p0 in range(0, ncols_total, 512):
            n = min(512, ncols_total - p0)
            if wres is None:
                wt, Rw = self.wtile(wsrc(p0, n), kcn, n)
            else:
                wt, Rw = wres[0][:, :, p0:p0 + n], wres[1]
            nch = n // 128
            bk = self.banks(nch)
            for ci in range(nch):
                ps = self.ps[bk[ci]][:, 0:width]
                for kc in range(kcn):
                    self.mm(ps, wt[:, kc, ci * 128:(ci + 1) * 128], rhs(kc), kc == 0, kc == kcn - 1,
                            [Rw] + R_rhs, [self.R_ps[bk[ci]]])
                cb(p0 // 128 + ci, ps, self.R_ps[bk[ci]])

    def lin_tm(self, wsrc, kcn, ncols_total, lhsT, R_l, nblk, rows, cb, wres=None):
        for p0 in range(0, ncols_total, 512):
            n = min(512, ncols_total - p0)
            if wres is None:
                wt, Rw = self.wtile(wsrc(p0, n), kcn, n)
            else:
                wt, Rw = wres[0][:, :, p0:p0 + n], wres[1]
            bk = self.banks(nblk)
            pend = None
            for blk in range(nblk):
                ps = self.ps[bk[blk]][0:rows, 0:n]
                for kc in range(kcn):
                    self.mm(ps, lhsT(kc, blk), wt[:, kc, 0:n], kc == 0, kc == kcn - 1,
                            [Rw] + (R_l(blk) if callable(R_l) else R_l), [self.R_ps[bk[blk]]])
                if pend is not None:
                    cb(*pend)
                pend = (p0, n, blk, ps, self.R_ps[bk[blk]])
            cb(*pend)

    def rope(self, out_f32, ps, cosv, sinv, nh, rows, reads, R_out, scale=None):
        w = nh * 64
        t1 = self.rt1[0:rows, 0:w]
        t2 = self.rt2[0:rows, 0:w]
        self.tt("dve", t1, ps, cosv, ALU.mult, reads, [self.R_rt])
        p3 = ps.rearrange("p (h x) -> p h x", h=nh)
        s3 = sinv.rearrange("p (h x) -> p h x", h=nh)
        t23 = t2.rearrange("p (h x) -> p h x", h=nh)
        self.tt("dve", t23[:, :, 0:32], p3[:, :, 32:64], s3[:, :, 0:32], ALU.mult, reads, [self.R_rt])
        self.tt("dve", t23[:, :, 32:64], p3[:, :, 0:32], s3[:, :, 32:64], ALU.mult, reads, [self.R_rt])
        if scale is None:
            self.tt("dve", out_f32, t1, t2, ALU.add, [self.R_rt], [R_out])
        else:
            self.tt("dve", t1, t1, t2, ALU.add, [self.R_rt], [self.R_rt])
            self.ts("dve", out_f32, t1, scale, None, ALU.mult, None, [self.R_rt], [R_out])

    def tr_to(self, dst, src16, nchunk, rows, Rsrc, Rdst, dpart=128, eng="act"):
        bk = self.banks(1)[0]
        pb = self.psb[bk]
        for j in range(nchunk):
            self.tr(pb[0:dpart, j * 128: j * 128 + rows], src16[0:rows, j * dpart:(j + 1) * dpart],
                    self.ident[0:rows, 0:rows], [Rsrc], [self.R_ps[bk]])
        srcv = pb[0:dpart, 0:nchunk * 128].rearrange("p (a b) -> p a b", a=nchunk)[:, :, 0:rows]
        self.copy(eng, dst, srcv, [self.R_ps[bk]], [Rdst])

    def kv_hT_norm(self, x, blk, rows):
        xb = self.xblk[self.xbi % 2]
        Rx = self.R_xblk[self.xbi % 2]
        self.xbi += 1
        self.dma("sp", xb[0:rows], x[blk * rows:(blk + 1) * rows, :], [], [Rx])
        return self._hT_norm(xb[0:rows], Rx, rows)

    def kv_hT(self, x, nblk, rows, sel):
        for blk in range(nblk):
            info = self.kv_hT_norm(x, blk, rows)
            self._hT_tr(info, blk, rows, self.hTA[sel], self.R_hTA[sel][blk])

    def kv_proj(self, part, sel, nblk, rows, src, scr, key0, outs):
        for f in self.kv_panels(sel, nblk, rows, src, scr, key0, outs)[(0 if part == 1 else 4):(4 if part == 1 else 6)]:
            f()

    def kv_panels(self, sel, nblk, rows, src, scr, key0, outs):
        W = nblk * rows
        blk0 = key0 // 128
        w_in = self.w["w_in"]
        hT = self.hTA[sel]
        R_hT = self.R_hTA[sel]

        def win(c0):
            return lambda p0, n: w_in[:, c0 + p0: c0 + p0 + n].rearrange("(kc p) n -> p kc n", p=128)

        def lhs(kc, blk):
            return hT[:, kc, blk * rows:(blk + 1) * rows]

        def Rl(blk):
            return [R_hT[blk]]

        def out_stage(ps, Rps, n):
            o = self.ostg[self.oi % 6]
            Ro = self.R_ostg[self.oi % 6]
            self.oi += 1
            return o, Ro

        def k16_next():
            i = self.k16i % 2
            self.k16i += 1
            return self.k16r[i], self.R_k16r[i]

        Rs = self.R_scr
        if True:
            def cb_k(p0, n, blk, ps, Rps):
                self.flush(2)
                o, Ro = out_stage(ps, Rps, n)
                self.copy("act", o[0:rows, 0:n], ps, [Rps], [Ro])
                if outs is not None:
                    self.defer(lambda: self.dma("sp", outs["nk"][blk * rows:(blk + 1) * rows, p0:p0 + n],
                                                o[0:rows, 0:n], [Ro], [], final=True))
                k16, Rk = k16_next()
                self.copy("dve", k16[0:rows, 0:n], o[0:rows, 0:n], [Ro], [Rk])
                h0 = p0 // 128
                self.tr_to(self.KTstg[:, h0:h0 + 4, blk * rows:(blk + 1) * rows], k16, 4, rows, Rk,
                           self.R_KTstg[p0 // 512][blk], eng="dve" if blk % 2 else "act")

            def run_k(pb):
                self.lin_tm(win(1024 + pb), 16, 512, lhs, Rl, nblk, rows,
                            lambda p0, n, blk, ps, Rps: cb_k(pb, n, blk, ps, Rps))
                if pb == 512:
                    self.dma("sp", scr["KTsb"][:, :, key0:key0 + W].rearrange("h d k -> d h k"),
                             self.KTstg[:, :, 0:W], [r for g in self.R_KTstg for r in g[0:nblk]], [])

            def cb_v(p0, n, blk, ps, Rps):
                self.flush(2)
                o, Ro = out_stage(ps, Rps, n)
                self.copy("act", o[0:rows, 0:n], ps, [Rps], [Ro])
                if outs is not None:
                    self.defer(lambda: self.dma("sp", outs["nv"][blk * rows:(blk + 1) * rows, p0:p0 + n],
                                                o[0:rows, 0:n], [Ro], [], final=True))
                h0 = p0 // 128
                self.copy("dve", self.Vstg[0:rows, h0:h0 + 4, blk, :],
                          o[0:rows, 0:n].rearrange("p (h v) -> p h v", h=4), [Ro], [self.R_Vstg[p0 // 512][blk]])

            def run_v(pb):
                self.lin_tm(win(2048 + pb), 16, 512, lhs, Rl, nblk, rows,
                            lambda p0, n, blk, ps, Rps: cb_v(pb, n, blk, ps, Rps))
                if pb == 512:
                    self.dma("sp", scr["Vsb"][:, 0:rows, blk0:blk0 + nblk, :].rearrange("h p b v -> p h (b v)"),
                             self.Vstg[0:rows, :, 0:nblk, :].rearrange("p h b v -> p h (b v)"),
                             [r for g in self.R_Vstg for r in g[0:nblk]], [])

        def cb_c(p0, n, blk, ps, Rps):
            self.flush(2)
            o, Ro = out_stage(ps, Rps, n)
            rstd, Rm = self.norm_stats(ps, rows, 512, [Rps])
            self.stt(o[0:rows, 0:512], ps, rstd, self.gkv[0:rows], ALU.mult, ALU.mult, [Rps, Rm], [Ro])
            if outs is not None:
                self.defer(lambda: self.dma("sp", outs["nckv"][blk * rows:(blk + 1) * rows, :], o[0:rows, 0:512],
                                            [Ro], [], final=True))
            k16, Rk = k16_next()
            self.copy("act", k16[0:rows, 0:512], o[0:rows, 0:512], [Ro], [Rk])
            self.tr_to(self.ckvT[:, :, blk * rows:(blk + 1) * rows], k16, 4, rows, Rk, self.R_ckvT[blk],
                       eng="dve" if blk % 2 else "act")

        def run_c():
            self.lin_tm(win(3584), 16, 512, lhs, Rl, nblk, rows, cb_c)

        def cb_r(p0, n, blk, ps, Rps):
            self.flush(2)
            o, Ro = out_stage(ps, Rps, n)
            self.dma("sp", self.tabc[0:rows, 0:64], src["cos"][blk * rows:(blk + 1) * rows, :], [], [self.R_tab])
            self.dma("sp", self.tabs[0:rows, 0:64], src["sin"][blk * rows:(blk + 1) * rows, :], [], [self.R_tab])
            self.rope(o[0:rows, 0:64], ps, self.tabc[0:rows, 0:64], self.tabs[0:rows, 0:64], 1, rows,
                      [Rps, self.R_tab], Ro)
            if outs is not None:
                self.defer(lambda: self.dma("sp", outs["nkr"][blk * rows:(blk + 1) * rows, :], o[0:rows, 0:64],
                                            [Ro], [], final=True))
            k16, Rk = k16_next()
            self.copy("act", k16[0:rows, 0:64], o[0:rows, 0:64], [Ro], [Rk])
            self.tr_to(self.KTrstg[0:64, :, blk * rows:(blk + 1) * rows], k16, 1, rows, Rk, self.R_KTrstg[blk],
                       dpart=64, eng="dve")

        def run_rest():
            self.lin_tm(None, 16, 64, lhs, Rl, nblk, rows, cb_r, wres=(self.wr_kr, self.R_wres))
            self.dma("sp", scr["KTr"][:, key0:key0 + W], self.KTrstg[0:64, 0, 0:W], self.R_KTrstg[0:nblk], [])
            self.kv_up(nblk, rows, scr, key0)
        return [lambda: run_k(0), lambda: run_k(512), lambda: run_v(0), lambda: run_v(512), run_c, run_rest]

    def kv_up(self, nblk, rows, scr, key0):
        W = nblk * rows
        blk0 = key0 // 128
        w_uk = self.w["w_uk"]
        w_uv = self.w["w_uv"]
        Rs = self.R_scr

        def cb_kn(c, ps, Rps):
            self.copy("act" if c % 2 == 0 else "dve", self.KTnstg[:, c, 0:W], ps, [Rps], [self.R_KTnstg[c]])
        self.lin_fm(None, 4, 1024, lambda kc: self.ckvT[:, kc, 0:W], self.R_ckvT[0:nblk], W, cb_kn,
                    wres=(self.wr_uk, self.R_wres))
        self.dma("sp", scr["KTn"][:, :, key0:key0 + W].rearrange("h d k -> d h k"), self.KTnstg[:, :, 0:W],
                 self.R_KTnstg, [])

        def cb_vm(p0, n, blk, ps, Rps):
            h0 = p0 // 128
            self.copy("act" if blk % 2 == 0 else "dve", self.Vmstg[0:rows, h0:h0 + 4, blk, :],
                      ps.rearrange("p (h v) -> p h v", h=4), [Rps], [self.R_Vmstg[p0 // 512][blk]])
        self.lin_tm(None, 4, 1024,
                    lambda kc, blk: self.ckvT[:, kc, blk * rows:(blk + 1) * rows], lambda blk: [self.R_ckvT[blk]],
                    nblk, rows, cb_vm, wres=(self.wr_uv, self.R_wres))
        self.dma("sp", scr["Vm"][:, 0:rows, blk0:blk0 + nblk, :].rearrange("h p b v -> p h (b v)"),
                 self.Vmstg[0:rows, :, 0:nblk, :].rearrange("p h b v -> p h (b v)"),
                 [r for g in self.R_Vmstg for r in g[0:nblk]], [])

    def kv_cache_tile(self, src, scr, key0):
        nblk, rows = 4, 128
        W = 512
        blk0 = key0 // 128
        Rs = self.R_scr
        for blk in range(nblk):
            r0 = key0 + blk * rows
            kb, Rkb = self.k16b[blk % 2], self.R_k16b[blk % 2]
            self.dma("pool", kb[0:rows, :], src["k"][r0:r0 + rows, :], [], [Rkb])
            for hh in range(2):
                self.tr_to(self.KTstg[:, hh * 4:hh * 4 + 4, blk * rows:(blk + 1) * rows],
                           kb[:, hh * 512:(hh + 1) * 512], 4, rows, Rkb, self.R_KTstg[hh][blk],
                           eng="act" if hh == 0 else "dve")
            self.dma("pool", self.Vstg[0:rows, :, blk, :],
                     src["v"][r0:r0 + rows, :].rearrange("p (h v) -> p h v", h=8), [],
                     [self.R_Vstg[0][blk], self.R_Vstg[1][blk]])
            k16, Rk = self.k16r[self.k16i % 2], self.R_k16r[self.k16i % 2]
            self.k16i += 1
            self.dma("pool", k16[0:rows, 0:512], src["ckv"][r0:r0 + rows, :], [], [Rk])
            self.tr_to(self.ckvT[:, :, blk * rows:(blk + 1) * rows], k16, 4, rows, Rk, self.R_ckvT[blk])
            self.dma("pool", self.r16[0:rows, 0:64], src["kr"][r0:r0 + rows, :], [], [self.R_r16])
            self.tr_to(self.KTrstg[0:64, :, blk * rows:(blk + 1) * rows], self.r16, 1, rows, self.R_r16,
                       self.R_KTrstg[blk], dpart=64, eng="dve")
        self.dma("sp", scr["KTsb"][:, :, key0:key0 + W].rearrange("h d k -> d h k"), self.KTstg[:, :, 0:W],
                 [r for g in self.R_KTstg for r in g], [])
        self.dma("sp", scr["Vsb"][:, :, blk0:blk0 + nblk, :].rearrange("h p b v -> p h (b v)"),
                 self.Vstg.rearrange("p h b v -> p h (b v)"), [r for g in self.R_Vstg for r in g], [])
        self.dma("sp", scr["KTr"][:, key0:key0 + W], self.KTrstg[0:64, 0, 0:W], self.R_KTrstg, [])
        self.kv_up(nblk, rows, scr, key0)

    def _hT_norm(self, xa, Rx, rows):
        rstd, Rm = self.norm_stats(xa, rows, D, [Rx])
        hb = self.hbr[self.hbi % 2]
        Rhb = self.R_hbr[self.hbi % 2]
        self.hbi += 1
        self.stt(hb[0:rows], xa, rstd, self.gbuf[0:rows], ALU.mult, ALU.mult, [Rx, Rm, self.R_gbuf], [Rhb])
        return hb, Rhb

    def _hT_tr(self, hbinfo, blk, rows, hT, R_h):
        hb, Rhb = hbinfo
        bk = self.banks(2)
        for half in range(2):
            pb = self.psb[bk[half]]
            for j in range(8):
                kc = half * 8 + j
                self.tr(pb[:, j * 128: j * 128 + rows], hb[0:rows, kc * 128:(kc + 1) * 128],
                        self.ident[0:rows, 0:rows], [Rhb], [self.R_ps[bk[half]]])
            srcv = pb[:, 0:1024].rearrange("p (a b) -> p a b", a=8)[:, :, 0:rows]
            dst = hT[:, half * 8:(half + 1) * 8, blk * rows:(blk + 1) * rows]
            self.copy("act" if half == 0 else "dve", dst, srcv, [self.R_ps[bk[half]]], [R_h])

    def _hT_block(self, xa, Rx, blk, rows, hT=None, R_h=None):
        if hT is None:
            hT, R_h = self.hT, self.R_hT[blk]
        self._hT_tr(self._hT_norm(xa, Rx, rows), blk, rows, hT, R_h)

    def q_tile(self, nblk, rows, xsrc, psrc, cosq, sinq, segs, ydst, mask_par):
        W = nblk * rows
        P = self.P
        w_in = self.w["w_in"]
        X = self.X
        R_X = self.R_X

        def win(c0):
            return lambda p0, n: w_in[:, c0 + p0: c0 + p0 + n].rearrange("(kc p) n -> p kc n", p=128)

        def wfull(w):
            return lambda p0, n: w[:, p0:p0 + n].rearrange("(kc p) n -> p kc n", p=128)

        self.dma("sp", self.gbuf, self.g["g_mix_pre"], [], [self.R_gbuf])
        for blk in range(nblk):
            self.dma("sp", X[0:rows, blk, :], xsrc[blk * rows:(blk + 1) * rows, :], [], [R_X[blk]])
            self._hT_block(X[0:rows, blk, :], R_X[blk], blk, rows)
        R_h = self.R_hT[0:nblk]

        def hrhs(kc):
            return self.hT[:, kc, 0:W]

        def hlhs(kc, blk):
            return self.hT[:, kc, blk * rows:(blk + 1) * rows]

        self.dma("sp", self.gq, self.gq_d, [], [self.R_gq])
        def cb_cq(p0, n, blk, ps, Rps):
            rstd, Rm = self.norm_stats(ps, rows, 512, [Rps])
            self.stt(self.k16[0:rows, 0:512], ps, rstd, self.gq[0:rows], ALU.mult, ALU.mult,
                     [Rps, Rm, self.R_gq], [self.R_k16])
            bk = self.banks(1)[0]
            pb = self.psb[bk]
            for j in range(4):
                self.tr(pb[:, j * 128: j * 128 + rows], self.k16[0:rows, j * 128:(j + 1) * 128],
                        self.ident[0:rows, 0:rows], [self.R_k16], [self.R_ps[bk]])
            srcv = pb[:, 0:512].rearrange("p (a b) -> p a b", a=4)[:, :, 0:rows]
            self.copy("act", self.cqT[:, :, blk * rows:(blk + 1) * rows], srcv, [self.R_ps[bk]], [self.R_cqT])
        self.lin_tm(win(3072), 16, 512, hlhs, R_h, nblk, rows, cb_cq)

        w_uq = self.w["w_uq"]
        uq4 = w_uq.rearrange("(kc p) (h e) -> p kc h e", p=128, e=192)

        def cb_qn(c, ps, Rps):
            qs = self.qstg[self.qi % 16]
            Rq = self.R_qstg[self.qi % 16]
            self.qi += 1
            self.ts("dve", qs[:, 0:W], ps, MLA_SCALE, None, ALU.mult, None, [Rps], [Rq])
            self.dma("sp", self.Qscr[2, c, :, 0:W], qs[:, 0:W], [Rq], [])
        self.lin_fm(lambda p0, n: uq4[:, :, p0 // 128:(p0 + n) // 128, 0:128], 4, 1024,
                    lambda kc: self.cqT[:, kc, 0:W], [self.R_cqT], W, cb_qn)

        def cb_qr(p0, n, blk, ps, Rps):
            self.dma("sp", self.tabc[0:rows, :], cosq[blk * rows:(blk + 1) * rows, :], [], [self.R_tab])
            self.dma("sp", self.tabs[0:rows, :], sinq[blk * rows:(blk + 1) * rows, :], [], [self.R_tab])
            self.rope(self.k16[0:rows, 0:512], ps, self.tabc[0:rows, :], self.tabs[0:rows, :], 8, rows,
                      [Rps, self.R_tab], self.R_k16, scale=MLA_SCALE)
            bk = self.banks(1)[0]
            pb = self.psb[bk]
            for j in range(8):
                self.tr(pb[0:64, j * 128: j * 128 + rows], self.k16[0:rows, j * 64:(j + 1) * 64],
                        self.ident[0:rows, 0:rows], [self.R_k16], [self.R_ps[bk]])
            srcv = pb[0:64, 0:1024].rearrange("p (a b) -> p a b", a=8)[:, :, 0:rows]
            self.copy("act", self.qrstg[0:64, :, blk * rows:(blk + 1) * rows], srcv, [self.R_ps[bk]],
                      [self.R_qrstg])
        self.lin_tm(lambda p0, n: uq4[:, :, :, 128:192], 4, 512,
                    lambda kc, blk: self.cqT[:, kc, blk * rows:(blk + 1) * rows], [self.R_cqT],
                    nblk, rows, cb_qr)
        self.dma("sp", self.Qscr[3, :, 0:64, 0:W].rearrange("h p t -> p h t"), self.qrstg[0:64, :, 0:W],
                 [self.R_qrstg], [])

        def cb_q(c, ps, Rps):
            qs = self.qstg[self.qi % 16]
            Rq = self.R_qstg[self.qi % 16]
            self.qi += 1
            self.ts("dve", qs[:, 0:W], ps, SB_SCALE, None, ALU.mult, None, [Rps], [Rq])
            self.dma("sp", self.Qscr[0, c, :, 0:W], qs[:, 0:W], [Rq], [])
            qs2 = self.qstg[self.qi % 16]
            Rq2 = self.R_qstg[self.qi % 16]
            self.qi += 1
            self.act(qs2[:, 0:W], ps, AF.Copy, [Rps], [Rq2], scale=-SB_SCALE)
            self.dma("sp", self.Qscr[1, c, :, 0:W], qs2[:, 0:W], [Rq2], [])
        self.lin_fm(win(0), 16, 1024, hrhs, R_h, W, cb_q)


        self.P.barrier()
        if mask_par is not None:
            self.dma("pool", self.mSB, self.masks["sb"][mask_par].rearrange("r p q -> p r q"), [], [self.R_mask["sb"]])
            self.dma("pool", self.mML, self.masks["ml"][mask_par].rearrange("r p q -> p r q"), [], [self.R_mask["ml"]])
        for seg in segs:
            self.attention(seg)

        if DEBUG and self.dbg_tile:
            self.dma("pool", self.dbgO[0], self.OsbT, [self.R_O], [], final=True)
            self.dma("pool", self.dbgO[1], self.OmlaT, [self.R_O], [], final=True)
        self.P.barrier()
        wb = self.w["w_branch"]
        for p0 in range(0, D, 512):
            def cb_g(which):
                def f(c, ps, Rps):
                    ci = c - p0 // 128
                    self.act(self.sgp[which][:, ci, 0:W], ps, AF.Sigmoid, [Rps], [self.R_sgp[which]])
                return f
            for which in range(2):
                c0 = 4160 + which * D + p0
                self.lin_fm(lambda q0, n, c0=c0: w_in[:, c0:c0 + n].rearrange("(kc p) n -> p kc n", p=128),
                            16, 512, hrhs, R_h, W, lambda c, ps, Rps, which=which:
                            self.act(self.sgp[which][:, c, 0:W], ps, AF.Sigmoid, [Rps], [self.R_sgp[which]]))

            def cb_b0(c, ps, Rps):
                self.tt("dve", self.tp[:, c, 0:W], ps, self.sgp[0][:, c, 0:W], ALU.mult,
                        [Rps, self.R_sgp[0]], [self.R_tp])
            self.lin_fm(lambda q0, n: wb[0, :, p0:p0 + n].rearrange("(kc p) n -> p kc n", p=128), 8, 512,
                        lambda kc: self.OsbT[:, kc, 0:W], [self.R_O], W, cb_b0)

            def cb_b1(c, ps, Rps):
                self.tt("dve", self.tp2[:, 0:W], ps, self.sgp[1][:, c, 0:W], ALU.mult,
                        [Rps, self.R_sgp[1]], [self.R_tp2])
                self.tt("dve", self.mergedT[:, p0 // 128 + c, 0:W], self.tp2[:, 0:W], self.tp[:, c, 0:W],
                        ALU.add, [self.R_tp2, self.R_tp], [self.R_merged])
            self.lin_fm(lambda q0, n: wb[1, :, p0:p0 + n].rearrange("(kc p) n -> p kc n", p=128), 8, 512,
                        lambda kc: self.OmlaT[:, kc, 0:W], [self.R_O], W, cb_b1)

        self.dma("sp", self.gbufB, self.g["g_mix_post"], [], [self.R_gbufB])

        def cb_store(p0, n, blk, ps, Rps):
            self.act(self.junk[0:rows, 0:n], ps, AF.Square, [Rps], [self.R_ssp[blk]],
                     accum=self.ssp[0:rows, blk * 4 + p0 // 512: blk * 4 + p0 // 512 + 1])
            self.tt("dve", self.stg[0:rows, blk, p0:p0 + n], ps, self.gbufB[0:rows, p0:p0 + n], ALU.mult,
                    [Rps, self.R_gbufB], [self.R_stg[blk]])
        self.lin_tm(wfull(self.w["w_out"]), 16, D, lambda kc, blk: self.mergedT[:, kc, blk * rows:(blk + 1) * rows],
                    [self.R_merged], nblk, rows, cb_store)
        if DEBUG and self.dbg_tile:
            self.dma("pool", self.dbgM, self.mergedT, [self.R_merged], [], final=True)
        self.norm_residual(nblk, rows)
        if DEBUG and self.dbg_tile:
            for blk in range(nblk):
                self.dma("sp", self.dbgX[0, blk * rows:(blk + 1) * rows, :], X[0:rows, blk, :], [R_X[blk]], [], final=True)

        self.P.barrier()
        self.dma("sp", self.gbuf, self.g["g_ffn_pre"], [], [self.R_gbuf])
        for blk in range(nblk):
            self._hT_block(X[0:rows, blk, :], R_X[blk], blk, rows)
        w_up = self.w["w_up"]
        w_down = self.w["w_down"]
        self.dma("sp", self.gbufB, self.g["g_ffn_post"], [], [self.R_gbufB])
        for q in range(4):
            def cb_up(c, ps, Rps):
                self.act(self.sq[:, 0:W], ps, AF.Square, [Rps], [self.R_sq])
                self.stt(self.uT[:, c, 0:W], ps, 0.0, self.sq[:, 0:W], ALU.is_gt, ALU.mult,
                         [Rps, self.R_sq], [self.R_uT])
            self.lin_fm(lambda p0, n, q=q: w_up[:, q * 2048 + p0: q * 2048 + p0 + n].rearrange(
                "(kc p) n -> p kc n", p=128), 16, 2048, hrhs, R_h, W, cb_up)

            def cb_dn(p0, n, blk, ps, Rps, q=q):
                if q == 0:
                    self.copy("act" if blk % 2 == 0 else "dve", self.stg[0:rows, blk, p0:p0 + n], ps, [Rps],
                              [self.R_stg[blk]])
                else:
                    sv = self.stg[0:rows, blk, p0:p0 + n]
                    self.tt("dve", sv, ps, sv, ALU.add, [Rps, self.R_stg[blk]], [self.R_stg[blk]])
                    if q == 3:
                        self.act(self.junk[0:rows, 0:n], sv, AF.Square, [self.R_stg[blk]], [self.R_ssp[blk]],
                                 accum=self.ssp[0:rows, blk * 4 + p0 // 512: blk * 4 + p0 // 512 + 1])
                        self.tt("dve", sv, sv, self.gbufB[0:rows, p0:p0 + n], ALU.mult,
                                [self.R_stg[blk], self.R_gbufB], [self.R_stg[blk]])
            self.lin_tm(lambda p0, n, q=q: w_down[q * 2048:(q + 1) * 2048, p0:p0 + n].rearrange(
                "(kc p) n -> p kc n", p=128), 16, D,
                lambda kc, blk: self.uT[:, kc, blk * rows:(blk + 1) * rows], [self.R_uT], nblk, rows, cb_dn)
        self.norm_residual(nblk, rows)
        if DEBUG and self.dbg_tile:
            for blk in range(nblk):
                self.dma("sp", self.dbgX[1, blk * rows:(blk + 1) * rows, :], X[0:rows, blk, :], [R_X[blk]], [], final=True)

        self.P.barrier()
        self.dma("sp", self.gbuf, self.g["g_ple_gate"], [], [self.R_gbuf])
        for blk in range(nblk):
            self._hT_block(X[0:rows, blk, :], R_X[blk], blk, rows)

        def cb_pg(p0, n, blk, ps, Rps):
            self.act(self.stg[0:rows, blk, p0:p0 + n], ps, AF.Sigmoid, [Rps], [self.R_stg[blk]])
        self.lin_tm(wfull(self.w["w_ple_gate"]), 16, D, hlhs, R_h, nblk, rows, cb_pg)
        for blk in range(nblk):
            self.dma("sp", self.pblk[0:rows, :], psrc[blk * rows:(blk + 1) * rows, :], [], [self.R_pblk])
            self.copy("dve", self.k16[0:rows, 0:256], self.pblk[0:rows, :], [self.R_pblk], [self.R_k16])
            bk = self.banks(1)[0]
            pb = self.psb[bk]
            for j in range(2):
                self.tr(pb[:, j * 128: j * 128 + rows], self.k16[0:rows, j * 128:(j + 1) * 128],
                        self.ident[0:rows, 0:rows], [self.R_k16], [self.R_ps[bk]])
            srcv = pb[:, 0:256].rearrange("p (a b) -> p a b", a=2)[:, :, 0:rows]
            self.copy("act", self.pT[:, :, blk * rows:(blk + 1) * rows], srcv, [self.R_ps[bk]], [self.R_pT])

        self.dma("sp", self.gbufB, self.g["g_ple_post"], [], [self.R_gbufB])

        def cb_pp(p0, n, blk, ps, Rps):
            sv = self.stg[0:rows, blk, p0:p0 + n]
            self.tt("dve", sv, ps, sv, ALU.mult, [Rps, self.R_stg[blk]], [self.R_stg[blk]])
            self.act(self.junk[0:rows, 0:n], sv, AF.Square, [self.R_stg[blk]], [self.R_ssp[blk]],
                     accum=self.ssp[0:rows, blk * 4 + p0 // 512: blk * 4 + p0 // 512 + 1])
            self.tt("dve", sv, sv, self.gbufB[0:rows, p0:p0 + n], ALU.mult,
                    [self.R_stg[blk], self.R_gbufB], [self.R_stg[blk]])
        self.lin_tm(wfull(self.w["w_ple"]), 2, D, lambda kc, blk: self.pT[:, kc, blk * rows:(blk + 1) * rows],
                    [self.R_pT], nblk, rows, cb_pp)
        self.norm_residual(nblk, rows)
        for blk in range(nblk):
            self.dma("sp", ydst[blk * rows:(blk + 1) * rows, :], X[0:rows, blk, :], [R_X[blk]], [], final=True)

    def norm_residual(self, nblk, rows):
        for blk in range(nblk):
            i = self.smi % 4
            self.smi += 1
            sm = self.smr[i]
            R = self.R_smr[i]
            self.P.add("dve", lambda e, sm=sm, blk=blk: e.reduce_sum(
                out=sm[0:rows, 0:1], in_=self.ssp[0:rows, blk * 4:(blk + 1) * 4], axis=mybir.AxisListType.X),
                [self.R_ssp[blk]], [R])
            rstd, Rm = self.rstd_tail(sm, R, rows, D)
            self.stt(self.X[0:rows, blk, :], self.stg[0:rows, blk, :], rstd, self.X[0:rows, blk, :],
                     ALU.mult, ALU.add, [self.R_stg[blk], Rm, self.R_X[blk]], [self.R_X[blk]])

    def attention(self, seg):
        c0, nq, scr, blocks = seg["c0"], seg["nq"], seg["scr"], seg["blocks"]
        cols = slice(c0, c0 + nq)
        nb = len(blocks)
        CH = 8
        nvalid = sum(b[1] for b in blocks)
        self.dma("sp", self.KTr[0:64, 0:nvalid], scr["KTr"][:, 0:nvalid], [self.R_scr], [self.R_KTr])
        order = sorted(blocks, key=lambda b: -b[0])
        nk_of = {b[0]: b[1] for b in blocks}
        nch = (nb + 7) // 8
        sizes = [nb // nch + (1 if c < nb % nch else 0) for c in range(nch)]
        assert min(sizes) >= 2
        units = []
        chunks = []
        for h in range(NH):
            i = 0
            for c in range(nch):
                lo = order[i + sizes[c] - 1][0]
                chunks.append((h, lo, sizes[c], len(units)))
                for _ in range(sizes[c]):
                    kb, nk, mrel = order[i]
                    units.append((h, i, kb, nk, mrel, len(chunks) - 1))
                    i += 1
        first_of_head = {h: h * nb for h in range(NH)}

        def load_kv(g):
            h, lo, n, _ = chunks[g]
            slot = g % 2
            R = self.R_kv[slot]
            kv = self.kv[slot]
            nkl = nk_of[lo + n - 1]
            nf = n if nkl == 128 else n - 1
            ncols = nf * 128 + (0 if nkl == 128 else nkl)
            self.dma("sp", kv["KT"][:, 0:ncols], scr["KTsb"][h, :, lo * 128:lo * 128 + ncols], [self.R_scr], [R["KT"]])
            self.dma("sp", kv["KTn"][:, 0:ncols], scr["KTn"][h, :, lo * 128:lo * 128 + ncols], [self.R_scr], [R["KTn"]])
            if nf:
                self.dma("sp", kv["V"][:, 0:nf, :], scr["Vsb"][h, :, lo:lo + nf, :], [self.R_scr], [R["V"]])
                self.dma("sp", kv["Vm"][:, 0:nf, :], scr["Vm"][h, :, lo:lo + nf, :], [self.R_scr], [R["Vm"]])
            if nf < n:
                self.dma("sp", kv["V"][0:nkl, nf, :], scr["Vsb"][h, 0:nkl, lo + nf, :], [self.R_scr], [R["V"]])
                self.dma("sp", kv["Vm"][0:nkl, nf, :], scr["Vm"][h, 0:nkl, lo + nf, :], [self.R_scr], [R["Vm"]])

        def load_q(h):
            qb = self.qh[h % 2]
            R = self.R_qh[h % 2]
            for t in range(3):
                self.dma("sp", qb[t][:, 0:nq], self.Qscr[t, h, :, cols], [self.R_Qscr], [R[t]])
            self.dma("sp", qb[3][0:64, 0:nq], self.Qscr[3, h, 0:64, cols], [self.R_Qscr], [R[3]])

        def prefetch(step):
            for g in range(len(chunks)):
                if chunks[g][3] + 1 == step and g + 1 < len(chunks):
                    load_kv(g + 1)
            for h in range(NH):
                if first_of_head[h] + 1 == step and h + 1 < NH:
                    load_q(h + 1)

        def ensure_kv(u):
            g = units[u][5]
            h, lo, n, _ = chunks[g]
            return self.kv[g % 2], self.R_kv[g % 2], lo

        def ensure_q(h):
            return self.qh[h % 2], self.R_qh[h % 2]

        load_kv(0)
        load_q(0)

        st = {}
        ones = self.ones
        tri = self.tri
        idn = self.ident
        nid = self.negid
        PZ, PG, PS_, POS, POM, PDEN = 0, (1, 2), (3, 4), 5, 6, 7

        def s1(u):
            h, i, kb, nk, mrel, _g = units[u]
            kv, Rkv, b0 = ensure_kv(u)
            qb, Rq = ensure_q(h)
            lb = kb - b0
            d = dict(kv=kv, Rkv=Rkv, lb=lb, qb=qb, Rq=Rq)
            st[u] = d
            z = self.ps[PZ][0:nk, 0:nq]
            Rz = self.R_ps[PZ]
            kt = kv["KT"][:, lb * 128: lb * 128 + nk]
            Rmsb = self.R_mask["S" if mrel == "S" else "sb"]
            self.mm(z, kt, qb[0][:, 0:nq], True, mrel is None, [Rkv["KT"], Rq[0]], [Rz])
            if mrel is not None:
                self.mm(z, nid[0:nk, 0:nk], self.mask_ap("sb", mrel, nk, nq), False, True, [Rmsb], [Rz])
            sb_ = PS_[u % 2]
            s = self.ps[sb_][0:nk, 0:nq]
            Rs = self.R_ps[sb_]
            self.mm(s, kv["KTn"][:, lb * 128: lb * 128 + nk], qb[2][:, 0:nq], True, False, [Rkv["KTn"], Rq[2]], [Rs])
            self.mm(s, self.KTr[0:64, kb * 128: kb * 128 + nk], qb[3][0:64, 0:nq], False, bool(mrel is None or seg.get("nomla")),
                    [self.R_KTr, Rq[3]], [Rs])
            if mrel is not None and not seg.get("nomla"):
                self.mm(s, nid[0:nk, 0:nk], self.mask_ap("ml", mrel, nk, nq), False, True, [self.R_mask["ml"]], [Rs])

        def s2(u):
            h, i, kb, nk, mrel, _g = units[u]
            z = self.ps[PZ][0:nk, 0:nq]
            e1 = self.e1[u % 2][0:nk, 0:nq]
            Lp = self.Lp[u % 2][0:nk, 0:nq]
            self.act(e1, z, AF.Exp, [self.R_ps[PZ]], [self.R_e1[u % 2]])
            self.act(Lp, e1, AF.Ln, [self.R_e1[u % 2]], [self.R_Lp[u % 2]], bias=1.0)
            sb_ = PS_[u % 2]
            self.act(self.Pm[u % 2][0:nk, 0:nq], self.ps[sb_][0:nk, 0:nq], AF.Exp, [self.R_ps[sb_]],
                     [self.R_Pm[u % 2]])

        def s3(u):
            h, i, kb, nk, mrel, _g = units[u]
            d = st[u]
            kv, Rkv, lb, qb, Rq = d["kv"], d["Rkv"], d["lb"], d["qb"], d["Rq"]
            gb = PG[u % 2]
            g = self.ps[gb][0:nk, 0:nq]
            Rg = self.R_ps[gb]
            Lp = self.Lp[u % 2][0:nk, 0:nq]
            self.mm(g, tri[0:nk, 0:nk], Lp, True, False, [self.R_Lp[u % 2]], [Rg])
            if i > 0:
                self.mm(g, ones[:, 0:nk], self.Lsum[:, 0:nq], False, False, [self.R_Lsum], [Rg])
            kt = kv["KT"][:, lb * 128: lb * 128 + nk]
            self.mm(g, kt, qb[1][:, 0:nq], False, mrel is None, [Rkv["KT"], Rq[1]], [Rg])
            if mrel is not None:
                self.mm(g, idn[0:nk, 0:nk], self.mask_ap("sb", mrel, nk, nq), False, True,
                        [self.R_mask["S" if mrel == "S" else "sb"]], [Rg])
            if i == 0 and nb > 1:
                self.P.add("pool", lambda e: e.memset(self.Lsum[:, 0:nq], 0.0), [], [self.R_Lsum])
            if i < nb - 1:
                self.tt("dve", self.Lsum[0:nk, 0:nq], self.Lsum[0:nk, 0:nq], Lp, ALU.add,
                        [self.R_Lsum, self.R_Lp[u % 2]], [self.R_Lsum])
            Pm = self.Pm[u % 2][0:nk, 0:nq]
            if DEBUG and self.dbg_tile and h == 0 and nq == 512:
                self.dma("pool", self.dbgP[kb], Pm, [self.R_Pm[u % 2]], [], final=True)
            self.mm(self.ps[POM][:, 0:nq], kv["Vm"][0:nk, lb, :], Pm, i == 0, i == nb - 1,
                    [Rkv["Vm"], self.R_Pm[u % 2]], [self.R_ps[POM]])
            self.mm(self.ps[PDEN][:, 0:nq], ones[0:nk, :], Pm, i == 0, i == nb - 1,
                    [self.R_Pm[u % 2]], [self.R_ps[PDEN]])
            if i == nb - 1:
                self.P.add("dve", lambda e: e.reciprocal(self.rden[:, 0:nq], self.ps[PDEN][:, 0:nq]),
                           [self.R_ps[PDEN]], [self.R_rden])
                self.tt("dve", self.OmlaT[:, h, cols], self.ps[POM][:, 0:nq], self.rden[:, 0:nq], ALU.mult,
                        [self.R_ps[POM], self.R_rden], [self.R_O])

        def s4(u):
            h, i, kb, nk, mrel, _g = units[u]
            gb = PG[u % 2]
            self.act(self.Wt[u % 2][0:nk, 0:nq], self.ps[gb][0:nk, 0:nq], AF.Exp, [self.R_ps[gb]],
                     [self.R_Wt[u % 2]], scale=-1.0)

        def s5(u):
            h, i, kb, nk, mrel, _g = units[u]
            d = st.pop(u)
            kv, Rkv, lb = d["kv"], d["Rkv"], d["lb"]
            if DEBUG and self.dbg_tile and h == 0 and nq == 512:
                self.dma("pool", self.dbgW[kb], self.Wt[u % 2][0:nk, 0:nq], [self.R_Wt[u % 2]], [], final=True)
            self.mm(self.ps[POS][:, 0:nq], kv["V"][0:nk, lb, :], self.Wt[u % 2][0:nk, 0:nq], i == 0, i == nb - 1,
                    [Rkv["V"], self.R_Wt[u % 2]], [self.R_ps[POS]])
            if i == nb - 1:
                self.copy("act", self.OsbT[:, h, cols], self.ps[POS][:, 0:nq], [self.R_ps[POS]], [self.R_O])

        n = len(units)
        for step in range(n + 2):
            if step < n:
                s1(step)
                s2(step)
            if 0 <= step - 1 < n:
                s3(step - 1)
                s4(step - 1)
            if 0 <= step - 2 < n:
                s5(step - 2)
            prefetch(step)

    def mask_ap(self, kind, mrel, nk, nq):
        if mrel == "S":
            return self.mS[0:nk, 0:nq]
        m = self.mSB if kind == "sb" else self.mML
        return m[0:nk, mrel, 0:nq]

    def build(self):
        nc = self.nc
        self.xfull = self.din("xfull", [SEQ, D])
        self.xown = self.din("xown", [2048, D])
        self.pown = self.din("pown", [2048, 256])
        self.xs = self.din("xs", [64, D])
        self.pss = self.din("pss", [64, 256])
        self.csk = self.din("csk", [2, PAST, 1024])
        self.csv = self.din("csv", [2, PAST, 1024])
        self.cckv = self.din("cckv", [2, PAST, 512])
        self.ckr = self.din("ckr", [2, PAST, 64])
        self.cosk = self.din("cosk", [SEQ, 64])
        self.sink = self.din("sink", [SEQ, 64])
        self.cosq = self.din("cosq", [2048, 512])
        self.sinq = self.din("sinq", [2048, 512])
        self.cosks = self.din("cosks", [64, 64])
        self.sinks = self.din("sinks", [64, 64])
        self.cosqs = self.din("cosqs", [64, 512])
        self.sinqs = self.din("sinqs", [64, 512])
        self.masks = {"sb": self.din("masksb", [2, 8, 128, 512]), "ml": self.din("maskml", [2, 8, 128, 512])}
        self.maskS_d = self.din("masks_s", [128, 64])
        self.consts_d = self.din("consts", [128, 512])
        self.w = {}
        for name, shp in (("w_in", [D, 8256]), ("w_uq", [512, 1536]), ("w_uk", [512, 1024]),
                          ("w_uv", [512, 1024]), ("w_branch", [2, 1024, D]), ("w_out", [D, D]),
                          ("w_up", [D, 8192]), ("w_down", [8192, D]), ("w_ple_gate", [D, D]),
                          ("w_ple", [256, D])):
            self.w[name] = self.din(name, shp)
        self.g = {}
        for name in ("g_mix_pre", "g_mix_post", "g_ffn_pre", "g_ffn_post", "g_ple_gate", "g_ple_post"):
            self.g[name] = self.din(name, [128, D])
        self.gq_d = self.din("g_q", [128, 512])
        self.gkv_d = self.din("g_kv", [128, 512])
        self.y_own = self.dout("y_own", [2048, D])
        self.y_s = self.dout("y_s", [64, D])
        self.o_full = {"nk": self.dout("nk_full", [SEQ, 1024]), "nv": self.dout("nv_full", [SEQ, 1024]),
                       "nckv": self.dout("nckv_full", [SEQ, 512]), "nkr": self.dout("nkr_full", [SEQ, 64])}
        self.o_s = {"nk": self.dout("nk_s", [64, 1024]), "nv": self.dout("nv_s", [64, 1024]),
                    "nckv": self.dout("nckv_s", [64, 512]), "nkr": self.dout("nkr_s", [64, 64])}
        ds = (lambda n, sh: self.dout(n, sh, BF16)) if DEBUG else self.dscr
        self.scrP = {"KTsb": ds("KTsb", [NH, 128, SEQ]), "Vsb": ds("Vsb", [NH, 128, 32, 128]),
                     "KTn": ds("KTn", [NH, 128, SEQ]), "Vm": ds("Vm", [NH, 128, 32, 128]),
                     "KTr": ds("KTr", [64, SEQ])}
        if DEBUG:
            self.dbgP = self.dout("dbgP", [8, 128, NT])
            self.dbgW = self.dout("dbgW", [8, 128, NT])
        self.scrS = []
        for i in range(2):
            self.scrS.append({"KTsb": self.dscr("KTsb_s%d" % i, [NH, 128, SKEYS]),
                              "Vsb": self.dscr("Vsb_s%d" % i, [NH, 128, 9, 128]),
                              "KTn": self.dscr("KTn_s%d" % i, [NH, 128, SKEYS]),
                              "Vm": self.dscr("Vm_s%d" % i, [NH, 128, 9, 128]),
                              "KTr": self.dscr("KTr_s%d" % i, [64, SKEYS])})
        if DEBUG:
            self.Qscr = self.dout("Qscr", [4, NH, 128, NT], BF16)
            self.dbgO = self.dout("dbgO", [2, 128, NH, NT])
            self.dbgM = self.dout("dbgM", [128, 16, NT])
            self.dbgX = self.dout("dbgX", [2, NT, D])
        else:
            self.Qscr = self.dscr("Qscr", [4, NH, 128, NT])

        with ExitStack() as es:
            KB = 1024
            self.arena_bytes = 200 * KB
            self.arena = es.enter_context(nc.sbuf_tensor("arena", [128, self.arena_bytes // 2], BF16))
            self.ps = [es.enter_context(nc.psum_tensor("ps%d" % i, [128, 512], F32)) for i in range(8)]
            self.psb = [p[:].bitcast(BF16) for p in self.ps]
            self.ps = [p[:] for p in self.ps]
            self.R_ps = [Res("ps%d" % i, excl=True) for i in range(8)]
            self.layout()
            self.P.planning = True
            self.program()
            self.P.planning = False
            self.wconv = [self.dscr("wc%d" % k, [128, kcn * ncols]) for k, (_, kcn, ncols) in enumerate(self.wspecs)]
            self.R_wconv = [Res("wc%d" % k) for k in range(len(self.wspecs))]
            self.R_cwin = [Res("cwin%d" % k) for k in range(4)]
            self.wi = 0
            self.wissued = 0
            self.bank_rr = 0
            self.oi = self.qi = self.kvslot = self.qslot = 0
            self.program()
            self.P.emit(nc, self.finals)
        return nc

    def layout(self):
        KB = 1024
        v = self.view
        o = 0
        cst = v(o, [512], BF16); o += 1 * KB
        self.ident = cst[:, 0:128]
        self.tri = cst[:, 128:256]
        self.ones = cst[:, 256:384]
        self.negid = cst[:, 384:512]
        self.R_cst = Res("cst")
        smv = v(o, [64], F32); o += 256
        self.smr = [smv[:, 2 * i:2 * i + 2] for i in range(4)]
        self.R_smr = [Res("sm%d" % i) for i in range(4)]
        self.mS = v(o, [64], BF16); o += 128
        self.hT = v(o, [16, 512], BF16); o += 16 * KB
        self.R_hT = [Res("hT%d" % i) for i in range(4)]
        self.wt_view = [v(o + i * 16 * KB, [16, 512], BF16) for i in range(2)]; o += 32 * KB
        self.R_wt = [Res("wt0"), Res("wt1")]
        self.gbuf = v(o, [2048], F32); o += 8 * KB
        self.R_gbuf = Res("gbuf")
        self.gbufB = v(o, [2048], F32); o += 8 * KB
        self.R_gbufB = Res("gbufB")
        self.ssp = smv[:, 16:32]
        self.R_ssp = [Res("ssp%d" % i) for i in range(4)]
        self.hbr = [v(o + i * 4 * KB, [2048], BF16) for i in range(2)]; o += 8 * KB
        self.R_hbr = [Res("hb0"), Res("hb1")]
        self.junk = v(o, [2048], BF16); o += 4 * KB
        self.k16 = v(o, [1024], BF16); o += 2 * KB
        self.R_k16 = Res("k16")
        base = o
        self.R_tab = Res("tab")
        self.R_rt = Res("rt")
        self.R_gq = Res("gq")
        self.xblk = [v(o + i * 8 * KB, [2048], F32) for i in range(2)]; o += 16 * KB
        self.R_xblk = [Res("xb0"), Res("xb1")]
        self.ostg = [v(o + i * 2 * KB, [512], F32) for i in range(6)]; o += 12 * KB
        self.R_ostg = [Res("os%d" % i) for i in range(6)]
        self.k16b = [v(o + i * 2 * KB, [1024], BF16) for i in range(2)]; o += 4 * KB
        self.R_k16b = [Res("k16b0"), Res("k16b1")]
        self.k16r = [v(o + i * 2 * KB, [1024], BF16) for i in range(2)]; o += 4 * KB
        self.R_k16r = [Res("k16r0"), Res("k16r1")]
        self.hTA = [self.hT, v(o, [16, 512], BF16)]; o += 16 * KB
        self.R_hTA = [self.R_hT, [Res("hTB%d" % i) for i in range(4)]]
        self.r16 = v(o, [64], BF16); o += 128
        self.R_r16 = Res("r16")
        self.KTstg = v(o, [8, 512], BF16); o += 8 * KB
        self.KTnstg = v(o, [8, 512], BF16); o += 8 * KB
        self.Vstg = v(o, [8, 4, 128], BF16); o += 8 * KB
        self.Vmstg = v(o, [8, 4, 128], BF16); o += 8 * KB
        self.KTrstg = v(o, [1, 512], BF16); o += 1 * KB
        self.ckvT = v(o, [4, 512], BF16); o += 4 * KB
        self.gkv = v(o, [512], F32); o += 2 * KB
        self.A_tab = (v(o, [512], F32), v(o + 2 * KB, [512], F32), v(o + 4 * KB, [512], F32),
                      v(o + 6 * KB, [512], F32)); o += 8 * KB
        self.wr_uk = v(o, [4, 1024], BF16); o += 8 * KB
        self.wr_uv = v(o, [4, 1024], BF16); o += 8 * KB
        self.wr_kr = v(o, [16, 64], BF16); o += 2 * KB
        self.R_wres = Res("wres")
        self.R_KTstg = [[Res() for _ in range(4)] for _ in range(2)]
        self.R_Vstg = [[Res() for _ in range(4)] for _ in range(2)]
        self.R_Vmstg = [[Res() for _ in range(4)] for _ in range(2)]
        self.R_KTnstg = [Res() for _ in range(8)]
        self.R_KTrstg = [Res() for _ in range(4)]
        self.R_ckvT = [Res() for _ in range(4)]
        endA = o
        o = base
        self.X = v(o, [4, 2048], F32); o += 32 * KB
        self.R_X = [Res("X%d" % i) for i in range(4)]
        self.OsbT = v(o, [8, 512], BF16); o += 8 * KB
        self.OmlaT = v(o, [8, 512], BF16); o += 8 * KB
        self.R_O = Res("O")
        ub = o
        self.cqT = v(o, [4, 512], BF16); o += 4 * KB
        self.R_cqT = Res("cqT")
        self.qstg = [v(o + i * KB, [512], BF16) for i in range(16)]; o += 16 * KB
        self.R_qstg = [Res() for _ in range(16)]
        self.qrstg = v(o, [8, 512], BF16); o += 8 * KB
        self.R_qrstg = Res()
        self.gq = v(o, [512], F32); o += 2 * KB
        self.B_tab = (v(o, [512], F32), v(o + 2 * KB, [512], F32), v(o + 4 * KB, [512], F32),
                      v(o + 6 * KB, [512], F32)); o += 8 * KB
        end2 = o
        o = ub
        self.qh = []
        for s in range(2):
            self.qh.append([v(o + t * KB, [512], BF16) for t in range(4)]); o += 4 * KB
        self.R_qh = [[Res() for _ in range(4)] for _ in range(2)]
        self.kv = []
        for s in range(2):
            self.kv.append({"KT": v(o, [1024], BF16), "KTn": v(o + 2 * KB, [1024], BF16),
                            "V": v(o + 4 * KB, [8, 128], BF16), "Vm": v(o + 6 * KB, [8, 128], BF16)})
            o += 8 * KB
        self.R_kv = [{k: Res() for k in ("KT", "KTn", "V", "Vm")} for _ in range(2)]
        self.KTr = v(o, [SEQ], BF16); o += 8 * KB
        self.R_KTr = Res()
        self.mSB = v(o, [8, 512], BF16); o += 8 * KB
        self.mML = v(o, [8, 512], BF16); o += 8 * KB
        self.R_mask = {"sb": Res(), "ml": Res(), "S": Res()}
        self.e1 = [v(o + i * 2 * KB, [512], F32) for i in range(2)]; o += 4 * KB
        self.Lp = [v(o + i * KB, [512], BF16) for i in range(2)]; o += 2 * KB
        self.Wt = [v(o + i * KB, [512], BF16) for i in range(2)]; o += 2 * KB
        self.Pm = [v(o + i * KB, [512], BF16) for i in range(2)]; o += 2 * KB
        self.Lsum = v(o, [512], BF16); o += 1 * KB
        self.rden = v(o, [512], F32); o += 2 * KB
        self.R_e1, self.R_Lp, self.R_Wt, self.R_Pm = ([Res(), Res()] for _ in range(4))
        self.R_Lsum, self.R_rden = Res(), Res()
        end3 = o
        o = ub
        self.stg = v(o, [4, 2048], F32); o += 32 * KB
        self.R_stg = [Res() for _ in range(4)]
        so = o
        self.mergedT = v(o, [16, 512], BF16); o += 16 * KB
        self.R_merged = Res()
        self.sgp = [v(o + i * 4 * KB, [4, 512], BF16) for i in range(2)]; o += 8 * KB
        self.R_sgp = [Res(), Res()]
        self.tp = v(o, [4, 512], F32); o += 8 * KB
        self.R_tp = Res()
        self.tp2 = v(o, [512], F32); o += 2 * KB
        self.R_tp2 = Res()
        end4 = o
        o = so
        self.uT = v(o, [16, 512], BF16); o += 16 * KB
        self.R_uT = Res()
        self.sq = v(o, [512], F32); o += 2 * KB
        self.R_sq = Res()
        o = so
        self.pblk = v(o, [256], F32); o += 1 * KB
        self.R_pblk = Res("pblk")
        self.pT = v(o, [2, 512], BF16); o += 2 * KB
        self.R_pT = Res("pT")
        self.R_scr = Res("scr")
        self.R_Qscr = Res("Qscr")
        self.tabset = None
        assert max(endA, end2, end3, end4) <= self.arena_bytes, (endA, end2, end3, end4)

    def use_tabs(self, t):
        self.tabc, self.tabs, self.rt1, self.rt2 = t

    def program(self):
        self.oi = self.qi = self.kvslot = self.qslot = 0
        self.smi = self.hbi = self.k16i = self.xbi = 0
        self.deferred = []
        self.use_tabs(self.A_tab)
        self.dma("pool", self.arena[:, 0:512], self.consts_d, [], [self.R_cst])
        self.dma("pool", self.mS, self.maskS_d, [], [self.R_cst])
        self.dma("sp", self.gkv, self.gkv_d, [], [self.R_cst])
        self.P.barrier()
        self.wq = "sp"
        self.wdone = set()
        self.wuses = {}
        self.wb_pending = None
        self.dma("sp", self.gbuf, self.g["g_mix_pre"], [], [self.R_gbuf])
        nA = 8 if STAGE != "A0" else 1

        def srcP(t):
            return {"cos": self.cosk[t * 512:(t + 1) * 512, :], "sin": self.sink[t * 512:(t + 1) * 512, :]}

        def outP(t):
            return {k: a[t * 512:(t + 1) * 512, :] for k, a in self.o_full.items()}
        w_in = self.w["w_in"]
        self.dma("pool", self.wr_uk, self.w["w_uk"].rearrange("(kc p) n -> p kc n", p=128), [], [self.R_wres])
        self.dma("pool", self.wr_uv, self.w["w_uv"].rearrange("(kc p) n -> p kc n", p=128), [], [self.R_wres])
        self.dma("pool", self.wr_kr, w_in[:, 4096:4160].rearrange("(kc p) n -> p kc n", p=128), [], [self.R_wres])
        bg = [k for k, sp in enumerate(self.wspecs) if sp[0].name == "w_up" and len(sp[0].shape) == 3]

        def bg_convert(n):
            for _ in range(n):
                if self.P.planning or not bg:
                    return
                k = bg.pop(0)
                src_ap, kcn, ncols = self.wspecs[k]
                dstb = self.wconv[k].rearrange("p (k n) -> p k n", k=kcn)
                self.dma("pool", dstb, src_ap, [], [self.R_wconv[k], self.R_cwin[k % 2]])
                self.wdone.add(k)
        self.kv_hT(self.xfull[0:512, :], 4, 128, 0)
        for t in range(nA):
            panels = self.kv_panels(t % 2, 4, 128, srcP(t), self.scrP, t * 512, outP(t))
            nxt = t + 1 < nA
            xn = self.xfull[(t + 1) * 512:(t + 2) * 512, :] if nxt else None
            for p in range(5):
                if nxt and p < 4:
                    info = self.kv_hT_norm(xn, p, 128)
                panels[p]()
                if nxt and p < 4:
                    self._hT_tr(info, p, 128, self.hTA[(t + 1) % 2], self.R_hTA[(t + 1) % 2][p])
            panels[5]()
            bg_convert(2)
        self.flush()
        if STAGE == "A0":
            return
        for i in range(2):
            for t in range(2):
                self.kv_cache_tile({"k": self.csk[i], "v": self.csv[i], "ckv": self.cckv[i], "kr": self.ckr[i]},
                                   self.scrS[i], t * 512)
            srcS = {"cos": self.cosks[i * 32:(i + 1) * 32, :], "sin": self.sinks[i * 32:(i + 1) * 32, :]}
            outS = {k: a[i * 32:(i + 1) * 32, :] for k, a in self.o_s.items()}
            self.kv_hT(self.xs[i * 32:(i + 1) * 32, :], 1, 32, 0)
            self.kv_proj(1, 0, 1, 32, srcS, self.scrS[i], PAST, outS)
            self.kv_proj(2, 0, 1, 32, srcS, self.scrS[i], PAST, outS)
            self.flush()
        self.P.barrier()
        if STAGE == "A":
            return
        self.use_tabs(self.B_tab)
        self.wq = "pool"
        for j in range(4 if STAGE != "B0" else 1):
            nblk = 4 * TMAX[j] + 4
            blocks = []
            for kb in range(nblk):
                rel = kb - (nblk - 8)
                blocks.append((kb, 128, rel if rel >= 0 else None))
            self.dbg_tile = (j == 0)
            self.q_tile(4, 128, self.xown[j * 512:(j + 1) * 512, :], self.pown[j * 512:(j + 1) * 512, :],
                        self.cosq[j * 512:(j + 1) * 512, :], self.sinq[j * 512:(j + 1) * 512, :],
                        [dict(c0=0, nq=512, scr=self.scrP, blocks=blocks)],
                        self.y_own[j * 512:(j + 1) * 512, :], j % 2)
            self.P.barrier()
        if STAGE == "B0":
            return
        segs = []
        for i in range(2):
            blocks = [(kb, 128, None) for kb in range(8)] + [(8, 32, "S")]
            segs.append(dict(c0=i * 32, nq=32, scr=self.scrS[i], blocks=blocks, nomla=True))
        self.q_tile(2, 32, self.xs, self.pss, self.cosqs, self.sinqs, segs, self.y_s, None)


_NC = None
STAGE = "all"
DEBUG = False


def _rope_tables(pos):
    half = 32
    freqs = (10000.0 ** (-np.arange(half, dtype=np.float32) / half)).astype(np.float32)
    ang = pos.astype(np.float32)[:, None] * freqs[None, :]
    c = np.cos(ang).astype(np.float32)
    s = np.sin(ang).astype(np.float32)
    return np.concatenate([c, c], 1), np.concatenate([-s, s], 1)


def _diag_mask(i, kind):
    m = np.zeros((128, 512), np.float32)
    s = np.arange(128)[:, None]
    for qb in range(4):
        t = np.arange(128)[None, :]
        if qb < i:
            blk = np.full((128, 128), MASKV, np.float32)
        elif qb > i:
            blk = np.zeros((128, 128), np.float32)
        else:
            if kind == "sb":
                vis = s < t
            else:
                vis = (s // 64) <= (t // 64)
            blk = np.where(vis, 0.0, MASKV).astype(np.float32)
        m[:, qb * 128:(qb + 1) * 128] = blk
    return m


def _mask_sets(kind):
    mx = np.zeros((8, 128, 512), np.float32)
    mn = np.zeros((8, 128, 512), np.float32)
    for i in range(4):
        mx[4 + i] = _diag_mask(i, kind)
        mn[i] = _diag_mask(i, kind)
        mn[4 + i] = MASKV
    return mx, mn


def kernel(**inp):
    global _NC
    f = lambda a: np.ascontiguousarray(np.asarray(a, dtype=np.float32))
    xp = f(inp["x_prompt"]); xsmp = f(inp["x_sample"])
    csk = f(inp["cache_sb_k"])[0].reshape(16, PAST, 1024)
    csv = f(inp["cache_sb_v"])[0].reshape(16, PAST, 1024)
    cckv = f(inp["cache_mla_ckv"])[0]
    ckr = f(inp["cache_mla_krope"])[0]
    pp = f(inp["p_prompt"])[0]; psm = f(inp["p_sample"])[0]
    if _NC is None:
        _NC = Builder().build()
    nc = _NC
    shared = {}
    for name in ("w_in", "w_uq", "w_uk", "w_uv", "w_branch", "w_out", "w_up", "w_down", "w_ple_gate", "w_ple"):
        shared[name] = f(inp[name])[0]
    for name in ("g_mix_pre", "g_mix_post", "g_ffn_pre", "g_ffn_post", "g_ple_gate", "g_ple_post"):
        shared[name] = np.ascontiguousarray(np.broadcast_to(f(inp[name])[0][None, :], (128, D)))
    shared["g_q"] = np.ascontiguousarray(np.broadcast_to(f(inp["g_q"])[0][None, :], (128, 512)))
    shared["g_kv"] = np.ascontiguousarray(np.broadcast_to(f(inp["g_kv"])[0][None, :], (128, 512)))
    ck, sk = _rope_tables(np.arange(SEQ))
    shared["cosk"], shared["sink"] = ck, sk
    cks, sks = _rope_tables(PAST + np.arange(DSEQ))
    shared["cosks"] = np.ascontiguousarray(np.tile(cks, (2, 1)))
    shared["sinks"] = np.ascontiguousarray(np.tile(sks, (2, 1)))
    shared["cosqs"] = np.ascontiguousarray(np.tile(np.tile(cks, (1, 8)), (2, 1)))
    shared["sinqs"] = np.ascontiguousarray(np.tile(np.tile(sks, (1, 8)), (2, 1)))
    eye = np.eye(128, dtype=np.float32)
    tri = (np.arange(128)[:, None] >= np.arange(128)[None, :]).astype(np.float32)
    shared["consts"] = np.ascontiguousarray(np.concatenate([eye, tri, np.ones((128, 128), np.float32), -eye], 1))
    mS = np.zeros((128, 64), np.float32)
    mS[0:32, 0:32] = np.where(np.arange(32)[:, None] < np.arange(32)[None, :], 0.0, MASKV)
    shared["masks_s"] = mS
    msets = {k: _mask_sets(k) for k in ("sb", "ml")}
    in_maps = []
    for c in range(8):
        b, r = c // 2, c % 2
        own = OWN[r]
        m = dict(shared)
        m["xfull"] = xp[b]
        m["xown"] = np.ascontiguousarray(np.concatenate([xp[b, t * 512:(t + 1) * 512] for t in own], 0))
        m["pown"] = np.ascontiguousarray(np.concatenate([pp[b, t * 512:(t + 1) * 512] for t in own], 0))
        m["xs"] = np.ascontiguousarray(xsmp[2 * c:2 * c + 2].reshape(64, D))
        m["pss"] = np.ascontiguousarray(psm[2 * c:2 * c + 2].reshape(64, 256))
        m["csk"] = np.ascontiguousarray(csk[2 * c:2 * c + 2])
        m["csv"] = np.ascontiguousarray(csv[2 * c:2 * c + 2])
        m["cckv"] = np.ascontiguousarray(cckv[2 * c:2 * c + 2])
        m["ckr"] = np.ascontiguousarray(ckr[2 * c:2 * c + 2])
        pos = np.concatenate([np.arange(t * 512, (t + 1) * 512) for t in own])
        cq, sq = _rope_tables(pos)
        m["cosq"] = np.ascontiguousarray(np.tile(cq, (1, 8)))
        m["sinq"] = np.ascontiguousarray(np.tile(sq, (1, 8)))
        for k in ("sb", "ml"):
            mx, mn = msets[k]
            par0 = mn if r == 0 else mx
            par1 = mx if r == 0 else mn
            m["mask" + k] = np.ascontiguousarray(np.stack([par0, par1], 0))
        in_maps.append(m)
    res = run_bass_kernel_spmd(nc, in_maps, core_ids=list(range(8)))
    R = res.results
    if DEBUG:
        global _DBG
        _DBG = R
    return _post(R)


def _post(R):
    yp = np.zeros((4, SEQ, D), np.float32)
    ys = np.zeros((16, DSEQ, D), np.float32)
    nkp = np.zeros((1, 4, SEQ, 8, 128), np.float32); nvp = np.zeros_like(nkp)
    nckvp = np.zeros((1, 4, SEQ, 512), np.float32); nkrp = np.zeros((1, 4, SEQ, 64), np.float32)
    nks = np.zeros((1, 16, DSEQ, 8, 128), np.float32); nvs = np.zeros_like(nks)
    nckvs = np.zeros((1, 16, DSEQ, 512), np.float32); nkrs = np.zeros((1, 16, DSEQ, 64), np.float32)
    for c in range(8):
        b, r = c // 2, c % 2
        for j, t in enumerate(OWN[r]):
            sl = slice(t * 512, (t + 1) * 512)
            yp[b, sl] = R[c]["y_own"][j * 512:(j + 1) * 512]
            nkp[0, b, sl] = R[c]["nk_full"][sl].reshape(512, 8, 128)
            nvp[0, b, sl] = R[c]["nv_full"][sl].reshape(512, 8, 128)
            nckvp[0, b, sl] = R[c]["nckv_full"][sl]
            nkrp[0, b, sl] = R[c]["nkr_full"][sl]
        ys[2 * c:2 * c + 2] = R[c]["y_s"].reshape(2, DSEQ, D)
        nks[0, 2 * c:2 * c + 2] = R[c]["nk_s"].reshape(2, DSEQ, 8, 128)
        nvs[0, 2 * c:2 * c + 2] = R[c]["nv_s"].reshape(2, DSEQ, 8, 128)
        nckvs[0, 2 * c:2 * c + 2] = R[c]["nckv_s"].reshape(2, DSEQ, 512)
        nkrs[0, 2 * c:2 * c + 2] = R[c]["nkr_s"].reshape(2, DSEQ, 64)
    return (yp, ys, nkp, nvp, nckvp, nkrp, nks, nvs, nckvs, nkrs)
```

```python
import numpy as np
from contextlib import ExitStack
import concourse.bass as bass
import concourse.mybir as mybir
from concourse.bass_utils import run_bass_kernel_spmd

F32 = mybir.dt.float32
BF16 = mybir.dt.bfloat16
AF = mybir.ActivationFunctionType
ALU = mybir.AluOpType

D = 2048
SEQ = 4096
NT = 512
PAST = 1024
DSEQ = 32
NH = 8
EPS = 1e-6
SB_SCALE = 128 ** -0.5
MLA_SCALE = 192 ** -0.5
OWN = {0: [0, 3, 4, 7], 1: [1, 2, 5, 6]}
TMAX = [1, 3, 5, 7]
MASKV = 30000.0
SKEYS = 1152

ENGS = ("sp", "act", "pool", "pe", "dve")
DMA_NS = 24


class Res:
    __slots__ = ("name", "w", "rc", "rd", "excl")

    def __init__(self, name="", excl=False):
        self.name = name
        self.excl = excl
        self.w = None
        self.rc = {}
        self.rd = []


class Op:
    __slots__ = ("eng", "fn", "deps", "is_dma", "need", "sem", "val", "dma_i")

    def __init__(self, eng, fn, is_dma):
        self.eng = eng
        self.fn = fn
        self.deps = ()
        self.is_dma = is_dma
        self.need = False
        self.sem = None
        self.val = None
        self.dma_i = None


class Prog:
    def __init__(self):
        self.ops = {e: [] for e in ENGS}
        self.ndma = {e: 0 for e in ENGS}
        self.planning = False
        self.bar = {}
        self.live_dma = []
        self.last = {}

    def add(self, eng, fn, reads=(), writes=(), dma=False):
        if self.planning:
            return None
        op = Op(eng, fn, dma)
        deps = set()
        for r in reads:
            if r.w is not None:
                deps.add(r.w)
            if r.excl:
                for e2, o2 in r.rc.items():
                    if e2 != eng:
                        deps.add(o2)
        for r in writes:
            if r.w is not None:
                deps.add(r.w)
            deps.update(r.rc.values())
            deps.update(r.rd)
        b = self.bar.pop(eng, None)
        if b:
            deps.update(b)
        op.deps = [d for d in deps if not (d.eng == "pe" and eng == "pe" and not d.is_dma and not dma)]
        for d in op.deps:
            d.need = True
        for r in reads:
            if dma:
                r.rd.append(op)
            else:
                r.rc[eng] = op
        for r in writes:
            r.w = op
            r.rc = {}
            r.rd = []
        if dma:
            op.dma_i = self.ndma[eng]
            self.ndma[eng] += 1
            op.need = True
            self.live_dma.append(op)
        else:
            self.last[eng] = op
        self.ops[eng].append(op)
        return op

    def barrier(self):
        if self.planning:
            return
        deps = list(self.live_dma) + list(self.last.values())
        self.live_dma = []
        for e in ENGS:
            self.bar[e] = list(deps)

    def emit(self, nc, finals):
        engmap = {"sp": "sync", "act": "scalar", "pool": "gpsimd", "pe": "tensor", "dve": "vector"}
        with ExitStack() as es:
            csem = {e: es.enter_context(nc.semaphore("c_" + e)) for e in ENGS}
            dsem = {}
            for e in ENGS:
                if self.ndma[e]:
                    dsem[e] = [es.enter_context(nc.semaphore("d_%s_%d" % (e, i)))
                               for i in range(min(DMA_NS, self.ndma[e]))]
            for e in ENGS:
                cnt = 0
                for op in self.ops[e]:
                    if op.is_dma:
                        op.sem = dsem[e][op.dma_i % DMA_NS]
                        op.val = 16 * (op.dma_i // DMA_NS + 1)
                    elif op.need:
                        cnt += 1
                        op.sem = csem[e]
                        op.val = cnt
            block = es.enter_context(nc.Block())

            def make(e):
                ops = self.ops[e]

                def body(eng):
                    waited = {}
                    for op in ops:
                        ws = {}
                        for d in op.deps:
                            k = id(d.sem)
                            if k not in ws or ws[k][1] < d.val:
                                ws[k] = (d.sem, d.val)
                        if op.is_dma and op.dma_i >= DMA_NS:
                            k = id(op.sem)
                            v = op.val - 16
                            if k not in ws or ws[k][1] < v:
                                ws[k] = (op.sem, v)
                        for k, (s, v) in ws.items():
                            if waited.get(k, 0) >= v:
                                continue
                            waited[k] = v
                            eng.wait_ge(s, v)
                        inst = op.fn(eng)
                        if op.is_dma:
                            inst.then_inc(op.sem, 16)
                        elif op.need:
                            inst.then_inc(op.sem, 1)
                    if e == "sp":
                        ws = {}
                        for op in finals:
                            k = id(op.sem)
                            if k not in ws or ws[k][1] < op.val:
                                ws[k] = (op.sem, op.val)
                        for k, (s, v) in ws.items():
                            if waited.get(k, 0) < v:
                                eng.wait_ge(s, v)
                return body

            for e in ENGS:
                if self.ops[e] or e == "sp":
                    getattr(block, engmap[e])(make(e))


class Builder:
    def __init__(self):
        self.nc = bass.Bass("TRN2", target_bir_lowering=False)
        self.P = Prog()
        self.finals = []
        self.wplan = []
        self.wkeys = {}
        self.wspecs = []
        self.wq = "sp"
        self.wi = 0
        self.wissued = 0
        self.bank_rr = 0

    def din(self, name, shape, dt=F32):
        return self.nc.dram_tensor(name, list(shape), dt, kind="ExternalInput").ap()

    def dout(self, name, shape, dt=F32):
        return self.nc.dram_tensor(name, list(shape), dt, kind="ExternalOutput").ap()

    def dscr(self, name, shape, dt=BF16):
        return self.nc.dram_tensor(name, list(shape), dt, kind="Internal").ap()

    def view(self, off, shape, dt):
        esz = 4 if dt == F32 else 2
        n = int(np.prod(shape))
        assert off % 4 == 0
        nb = n * esz
        assert off + nb <= self.arena_bytes, (off, nb)
        a = self.arena[:, off // 2: (off + nb) // 2]
        if dt == F32:
            a = a.bitcast(F32)
        if len(shape) == 2:
            a = a.rearrange("p (a b) -> p a b", a=shape[0])
        elif len(shape) == 3:
            a = a.rearrange("p (a b c) -> p a b c", a=shape[0], b=shape[1])
        return a

    def mm(self, ps, lhsT, rhs, start, stop, reads, writes):
        self.P.add("pe", lambda e: e.matmul(ps, lhsT=lhsT, rhs=rhs, start=start, stop=stop),
                   reads, writes)

    def tr(self, ps, in_, ident, reads, writes):
        self.P.add("pe", lambda e: e.transpose(ps, in_, ident), reads, writes)

    def act(self, out, in_, func, reads, writes, scale=None, bias=None, accum=None):
        kw = {}
        if scale is not None:
            kw["scale"] = scale
        if bias is not None:
            kw["bias"] = bias
        if accum is not None:
            kw["accum_out"] = accum
        self.P.add("act", lambda e: e.activation(out=out, in_=in_, func=func, **kw), reads, writes)

    def copy(self, eng, out, in_, reads, writes):
        if eng == "act":
            self.P.add("act", lambda e: e.activation(out=out, in_=in_, func=AF.Copy), reads, writes)
        else:
            self.P.add(eng, lambda e: e.tensor_copy(out, in_), reads, writes)

    def ts(self, eng, out, in0, s1, s2, op0, op1, reads, writes):
        if op1 is None:
            self.P.add(eng, lambda e: e.tensor_scalar(out=out, in0=in0, scalar1=s1, scalar2=None, op0=op0),
                       reads, writes)
        else:
            self.P.add(eng, lambda e: e.tensor_scalar(out=out, in0=in0, scalar1=s1, scalar2=s2,
                                                      op0=op0, op1=op1), reads, writes)

    def tt(self, eng, out, in0, in1, op, reads, writes):
        self.P.add(eng, lambda e: e.tensor_tensor(out=out, in0=in0, in1=in1, op=op), reads, writes)

    def stt(self, out, in0, scalar, in1, op0, op1, reads, writes):
        self.P.add("dve", lambda e: e.scalar_tensor_tensor(out=out, in0=in0, scalar=scalar, in1=in1,
                                                           op0=op0, op1=op1), reads, writes)

    def dma(self, eng, out, in_, reads, writes, final=False):
        op = self.P.add(eng, lambda e: e.dma_start(out=out, in_=in_), reads, writes, dma=True)
        if final and op is not None:
            self.finals.append(op)
        return op

    def banks(self, n):
        r = []
        for _ in range(n):
            r.append(self.bank_rr % 8)
            self.bank_rr += 1
        return r

    def wtile(self, src_ap, kcn, ncols):
        key = (src_ap.name, src_ap.offset, str(src_ap.ap))
        if self.P.planning:
            if key not in self.wkeys:
                self.wkeys[key] = len(self.wspecs)
                self.wspecs.append((src_ap, kcn, ncols))
            self.wplan.append((self.wkeys[key], kcn, ncols))
            return self.wt_view[0][:, 0:kcn, 0:ncols], self.R_wt[0]
        i = self.wi
        self.wi += 1
        while self.wissued <= min(i + 1, len(self.wplan) - 1):
            j = self.wissued
            k, k2, n2 = self.wplan[j]
            slot = j % 2
            dst = self.wt_view[slot][:, 0:k2, 0:n2]
            srcb = self.wconv[k].rearrange("p (k n) -> p k n", k=k2)
            if k not in self.wdone:
                uses = self.wuses.get(k, 0)
                self.wuses[k] = uses + 1
                do_wb = (k % 2 == 0) or uses >= 1
                if do_wb:
                    self.wdone.add(k)
                src_ap = self.wspecs[k][0]
                if len(src_ap.shape) == 4:
                    for kc in range(k2):
                        self.dma("pool", dst[:, kc, :].rearrange("p (h e) -> p h e", e=src_ap.shape[3]),
                                 src_ap[:, kc, :, :], [], [self.R_wt[slot]])
                else:
                    self.dma("pool", dst, src_ap, [], [self.R_wt[slot]])
                wb = (srcb, dst, slot, k) if do_wb else None
            else:
                self.dma(self.wq, dst, srcb, [self.R_wconv[k]], [self.R_wt[slot]])
                wb = None
            if self.wb_pending is not None:
                pb_src, pb_dst, pb_slot, pb_k = self.wb_pending
                self.dma("pool", pb_src, pb_dst, [self.R_wt[pb_slot]], [self.R_wconv[pb_k]])
            self.wb_pending = wb
            self.wissued += 1
        slot = i % 2
        return self.wt_view[slot][:, 0:kcn, 0:ncols], self.R_wt[slot]

    def convert_weights(self):
        for k, (src_ap, kcn, ncols) in enumerate(self.wspecs):
            dstb = self.wconv[k].rearrange("p (k n) -> p k n", k=kcn)
            R = self.R_wconv[k]
            Rwin = self.R_cwin[k % 4]
            if len(src_ap.shape) == 4:
                for kc in range(kcn):
                    self.dma("pool", dstb[:, kc, :].rearrange("p (h e) -> p h e", e=src_ap.shape[3]),
                             src_ap[:, kc, :, :], [], [R, Rwin])
            else:
                self.dma("pool", dstb, src_ap, [], [R, Rwin])

    def norm_stats(self, src, rows, n, reads):
        i = self.smi % 4
        self.smi += 1
        sm = self.smr[i]
        R = self.R_smr[i]
        ncol = src.shape[-1]
        self.act(self.junk[0:rows, 0:ncol], src, AF.Square, reads, [R], accum=sm[0:rows, 0:1])
        return self.rstd_tail(sm, R, rows, n)

    def rstd_tail(self, sm, R, rows, n):
        self.ts("dve", sm[0:rows, 1:2], sm[0:rows, 0:1], 1.0 / n, EPS, ALU.mult, ALU.add, [R], [R])
        self.act(sm[0:rows, 1:2], sm[0:rows, 1:2], AF.Sqrt, [R], [R])
        self.P.add("dve", lambda e: e.reciprocal(sm[0:rows, 1:2], sm[0:rows, 1:2]), [R], [R])
        return sm[0:rows, 1:2], R

    def defer(self, fn):
        self.deferred.append(fn)

    def flush(self, keep=0):
        while len(self.deferred) > keep:
            self.deferred.pop(0)()

    def lin_fm(self, wsrc, kcn, ncols_total, rhs, R_rhs, width, cb, wres=None):
        for p0 in range(0, ncols_total, 512):
            n = min(512, ncols_total - p0)
            if wres is None:
                wt, Rw = self.wtile(wsrc(p0, n), kcn, n)
            else:
                wt, Rw = wres[0][:, :, p0:p0 + n], wres[1]
            nch = n // 128
            bk = self.banks(nch)
            for ci in range(nch):
                ps = self.ps[bk[ci]][:, 0:width]
                for kc in range(kcn):
                    self.mm(ps, wt[:, kc, ci * 128:(ci + 1) * 128], rhs(kc), kc == 0, kc == kcn - 1,
                            [Rw] + R_rhs, [self.R_ps[bk[ci]]])
                cb(p0 // 128 + ci, ps, self.R_ps[bk[ci]])

    def lin_tm(self, wsrc, kcn, ncols_total, lhsT, R_l, nblk, rows, cb, wres=None):
        for p0 in range(0, ncols_total, 512):
            n = min(512, ncols_total - p0)
            if wres is None:
                wt, Rw = self.wtile(wsrc(p0, n), kcn, n)
            else:
                wt, Rw = wres[0][:, :, p0:p0 + n], wres[1]
            bk = self.banks(nblk)
            pend = None
            for blk in range(nblk):
                ps = self.ps[bk[blk]][0:rows, 0:n]
                for kc in range(kcn):
                    self.mm(ps, lhsT(kc, blk), wt[:, kc, 0:n], kc == 0, kc == kcn - 1,
                            [Rw] + (R_l(blk) if callable(R_l) else R_l), [self.R_ps[bk[blk]]])
                if pend is not None:
                    cb(*pend)
                pend = (p0, n, blk, ps, self.R_ps[bk[blk]])
            cb(*pend)

    def rope(self, out_f32, ps, cosv, sinv, nh, rows, reads, R_out, scale=None):
        w = nh * 64
        t1 = self.rt1[0:rows, 0:w]
        t2 = self.rt2[0:rows, 0:w]
        self.tt("dve", t1, ps, cosv, ALU.mult, reads, [self.R_rt])
        p3 = ps.rearrange("p (h x) -> p h x", h=nh)
        s3 = sinv.rearrange("p (h x) -> p h x", h=nh)
        t23 = t2.rearrange("p (h x) -> p h x", h=nh)
        self.tt("dve", t23[:, :, 0:32], p3[:, :, 32:64], s3[:, :, 0:32], ALU.mult, reads, [self.R_rt])
        self.tt("dve", t23[:, :, 32:64], p3[:, :, 0:32], s3[:, :, 32:64], ALU.mult, reads, [self.R_rt])
        if scale is None:
            self.tt("dve", out_f32, t1, t2, ALU.add, [self.R_rt], [R_out])
        else:
            self.tt("dve", t1, t1, t2, ALU.add, [self.R_rt], [self.R_rt])
            self.ts("dve", out_f32, t1, scale, None, ALU.mult, None, [self.R_rt], [R_out])

    def tr_to(self, dst, src16, nchunk, rows, Rsrc, Rdst, dpart=128, eng="act"):
        bk = self.banks(1)[0]
        pb = self.psb[bk]
        for j in range(nchunk):
            self.tr(pb[0:dpart, j * 128: j * 128 + rows], src16[0:rows, j * dpart:(j + 1) * dpart],
                    self.ident[0:rows, 0:rows], [Rsrc], [self.R_ps[bk]])
        srcv = pb[0:dpart, 0:nchunk * 128].rearrange("p (a b) -> p a b", a=nchunk)[:, :, 0:rows]
        self.copy(eng, dst, srcv, [self.R_ps[bk]], [Rdst])

    def kv_hT_norm(self, x, blk, rows):
        xb = self.xblk[self.xbi % 2]
        Rx = self.R_xblk[self.xbi % 2]
        self.xbi += 1
        self.dma("sp", xb[0:rows], x[blk * rows:(blk + 1) * rows, :], [], [Rx])
        return self._hT_norm(xb[0:rows], Rx, rows)

    def kv_hT(self, x, nblk, rows, sel):
        for blk in range(nblk):
            info = self.kv_hT_norm(x, blk, rows)
            self._hT_tr(info, blk, rows, self.hTA[sel], self.R_hTA[sel][blk])

    def kv_proj(self, part, sel, nblk, rows, src, scr, key0, outs):
        for f in self.kv_panels(sel, nblk, rows, src, scr, key0, outs)[(0 if part == 1 else 4):(4 if part == 1 else 6)]:
            f()

    def kv_panels(self, sel, nblk, rows, src, scr, key0, outs):
        W = nblk * rows
        blk0 = key0 // 128
        w_in = self.w["w_in"]
        hT = self.hTA[sel]
        R_hT = self.R_hTA[sel]

        def win(c0):
            return lambda p0, n: w_in[:, c0 + p0: c0 + p0 + n].rearrange("(kc p) n -> p kc n", p=128)

        def lhs(kc, blk):
            return hT[:, kc, blk * rows:(blk + 1) * rows]

        def Rl(blk):
            return [R_hT[blk]]

        def out_stage(ps, Rps, n):
            o = self.ostg[self.oi % 6]
            Ro = self.R_ostg[self.oi % 6]
            self.oi += 1
            return o, Ro

        def k16_next():
            i = self.k16i % 2
            self.k16i += 1
            return self.k16r[i], self.R_k16r[i]

        Rs = self.R_scr
        if True:
            def cb_k(p0, n, blk, ps, Rps):
                self.flush(2)
                o, Ro = out_stage(ps, Rps, n)
                self.copy("act", o[0:rows, 0:n], ps, [Rps], [Ro])
                if outs is not None:
                    self.defer(lambda: self.dma("sp", outs["nk"][blk * rows:(blk + 1) * rows, p0:p0 + n],
                                                o[0:rows, 0:n], [Ro], [], final=True))
                k16, Rk = k16_next()
                self.copy("dve", k16[0:rows, 0:n], o[0:rows, 0:n], [Ro], [Rk])
                h0 = p0 // 128
                self.tr_to(self.KTstg[:, h0:h0 + 4, blk * rows:(blk + 1) * rows], k16, 4, rows, Rk,
                           self.R_KTstg[p0 // 512][blk], eng="dve" if blk % 2 else "act")

            def run_k(pb):
                self.lin_tm(win(1024 + pb), 16, 512, lhs, Rl, nblk, rows,
                            lambda p0, n, blk, ps, Rps: cb_k(pb, n, blk, ps, Rps))
                if pb == 512:
                    self.dma("sp", scr["KTsb"][:, :, key0:key0 + W].rearrange("h d k -> d h k"),
                             self.KTstg[:, :, 0:W], [r for g in self.R_KTstg for r in g[0:nblk]], [])

            def cb_v(p0, n, blk, ps, Rps):
                self.flush(2)
                o, Ro = out_stage(ps, Rps, n)
                self.copy("act", o[0:rows, 0:n], ps, [Rps], [Ro])
                if outs is not None:
                    self.defer(lambda: self.dma("sp", outs["nv"][blk * rows:(blk + 1) * rows, p0:p0 + n],
                                                o[0:rows, 0:n], [Ro], [], final=True))
                h0 = p0 // 128
                self.copy("dve", self.Vstg[0:rows, h0:h0 + 4, blk, :],
                          o[0:rows, 0:n].rearrange("p (h v) -> p h v", h=4), [Ro], [self.R_Vstg[p0 // 512][blk]])

            def run_v(pb):
                self.lin_tm(win(2048 + pb), 16, 512, lhs, Rl, nblk, rows,
                            lambda p0, n, blk, ps, Rps: cb_v(pb, n, blk, ps, Rps))
                if pb == 512:
                    self.dma("sp", scr["Vsb"][:, 0:rows, blk0:blk0 + nblk, :].rearrange("h p b v -> p h (b v)"),
                             self.Vstg[0:rows, :, 0:nblk, :].rearrange("p h b v -> p h (b v)"),
                             [r for g in self.R_Vstg for r in g[0:nblk]], [])

        def cb_c(p0, n, blk, ps, Rps):
            self.flush(2)
            o, Ro = out_stage(ps, Rps, n)
            rstd, Rm = self.norm_stats(ps, rows, 512, [Rps])
            self.stt(o[0:rows, 0:512], ps, rstd, self.gkv[0:rows], ALU.mult, ALU.mult, [Rps, Rm], [Ro])
            if outs is not None:
                self.defer(lambda: self.dma("sp", outs["nckv"][blk * rows:(blk + 1) * rows, :], o[0:rows, 0:512],
                                            [Ro], [], final=True))
            k16, Rk = k16_next()
            self.copy("act", k16[0:rows, 0:512], o[0:rows, 0:512], [Ro], [Rk])
            self.tr_to(self.ckvT[:, :, blk * rows:(blk + 1) * rows], k16, 4, rows, Rk, self.R_ckvT[blk],
                       eng="dve" if blk % 2 else "act")

        def run_c():
            self.lin_tm(win(3584), 16, 512, lhs, Rl, nblk, rows, cb_c)

        def cb_r(p0, n, blk, ps, Rps):
            self.flush(2)
            o, Ro = out_stage(ps, Rps, n)
            self.dma("sp", self.tabc[0:rows, 0:64], src["cos"][blk * rows:(blk + 1) * rows, :], [], [self.R_tab])
            self.dma("sp", self.tabs[0:rows, 0:64], src["sin"][blk * rows:(blk + 1) * rows, :], [], [self.R_tab])
            self.rope(o[0:rows, 0:64], ps, self.tabc[0:rows, 0:64], self.tabs[0:rows, 0:64], 1, rows,
                      [Rps, self.R_tab], Ro)
            if outs is not None:
                self.defer(lambda: self.dma("sp", outs["nkr"][blk * rows:(blk + 1) * rows, :], o[0:rows, 0:64],
                                            [Ro], [], final=True))
            k16, Rk = k16_next()
            self.copy("act", k16[0:rows, 0:64], o[0:rows, 0:64], [Ro], [Rk])
            self.tr_to(self.KTrstg[0:64, :, blk * rows:(blk + 1) * rows], k16, 1, rows, Rk, self.R_KTrstg[blk],
                       dpart=64, eng="dve")

        def run_rest():
            self.lin_tm(None, 16, 64, lhs, Rl, nblk, rows, cb_r, wres=(self.wr_kr, self.R_wres))
            self.dma("sp", scr["KTr"][:, key0:key0 + W], self.KTrstg[0:64, 0, 0:W], self.R_KTrstg[0:nblk], [])
            self.kv_up(nblk, rows, scr, key0)
        return [lambda: run_k(0), lambda: run_k(512), lambda: run_v(0), lambda: run_v(512), run_c, run_rest]

    def kv_up(self, nblk, rows, scr, key0):
        W = nblk * rows
        blk0 = key0 // 128
        w_uk = self.w["w_uk"]
        w_uv = self.w["w_uv"]
        Rs = self.R_scr

        def cb_kn(c, ps, Rps):
            self.copy("act" if c % 2 == 0 else "dve", self.KTnstg[:, c, 0:W], ps, [Rps], [self.R_KTnstg[c]])
        self.lin_fm(None, 4, 1024, lambda kc: self.ckvT[:, kc, 0:W], self.R_ckvT[0:nblk], W, cb_kn,
                    wres=(self.wr_uk, self.R_wres))
        self.dma("sp", scr["KTn"][:, :, key0:key0 + W].rearrange("h d k -> d h k"), self.KTnstg[:, :, 0:W],
                 self.R_KTnstg, [])

        def cb_vm(p0, n, blk, ps, Rps):
            h0 = p0 // 128
            self.copy("act" if blk % 2 == 0 else "dve", self.Vmstg[0:rows, h0:h0 + 4, blk, :],
                      ps.rearrange("p (h v) -> p h v", h=4), [Rps], [self.R_Vmstg[p0 // 512][blk]])
        self.lin_tm(None, 4, 1024,
                    lambda kc, blk: self.ckvT[:, kc, blk * rows:(blk + 1) * rows], lambda blk: [self.R_ckvT[blk]],
                    nblk, rows, cb_vm, wres=(self.wr_uv, self.R_wres))
        self.dma("sp", scr["Vm"][:, 0:rows, blk0:blk0 + nblk, :].rearrange("h p b v -> p h (b v)"),
                 self.Vmstg[0:rows, :, 0:nblk, :].rearrange("p h b v -> p h (b v)"),
                 [r for g in self.R_Vmstg for r in g[0:nblk]], [])

    def kv_cache_tile(self, src, scr, key0):
        nblk, rows = 4, 128
        W = 512
        blk0 = key0 // 128
        Rs = self.R_scr
        for blk in range(nblk):
            r0 = key0 + blk * rows
            kb, Rkb = self.k16b[blk % 2], self.R_k16b[blk % 2]
            self.dma("pool", kb[0:rows, :], src["k"][r0:r0 + rows, :], [], [Rkb])
            for hh in range(2):
                self.tr_to(self.KTstg[:, hh * 4:hh * 4 + 4, blk * rows:(blk + 1) * rows],
                           kb[:, hh * 512:(hh + 1) * 512], 4, rows, Rkb, self.R_KTstg[hh][blk],
                           eng="act" if hh == 0 else "dve")
            self.dma("pool", self.Vstg[0:rows, :, blk, :],
                     src["v"][r0:r0 + rows, :].rearrange("p (h v) -> p h v", h=8), [],
                     [self.R_Vstg[0][blk], self.R_Vstg[1][blk]])
            k16, Rk = self.k16r[self.k16i % 2], self.R_k16r[self.k16i % 2]
            self.k16i += 1
            self.dma("pool", k16[0:rows, 0:512], src["ckv"][r0:r0 + rows, :], [], [Rk])
            self.tr_to(self.ckvT[:, :, blk * rows:(blk + 1) * rows], k16, 4, rows, Rk, self.R_ckvT[blk])
            self.dma("pool", self.r16[0:rows, 0:64], src["kr"][r0:r0 + rows, :], [], [self.R_r16])
            self.tr_to(self.KTrstg[0:64, :, blk * rows:(blk + 1) * rows], self.r16, 1, rows, self.R_r16,
                       self.R_KTrstg[blk], dpart=64, eng="dve")
        self.dma("sp", scr["KTsb"][:, :, key0:key0 + W].rearrange("h d k -> d h k"), self.KTstg[:, :, 0:W],
                 [r for g in self.R_KTstg for r in g], [])
        self.dma("sp", scr["Vsb"][:, :, blk0:blk0 + nblk, :].rearrange("h p b v -> p h (b v)"),
                 self.Vstg.rearrange("p h b v -> p h (b v)"), [r for g in self.R_Vstg for r in g], [])
        self.dma("sp", scr["KTr"][:, key0:key0 + W], self.KTrstg[0:64, 0, 0:W], self.R_KTrstg, [])
        self.kv_up(nblk, rows, scr, key0)

    def _hT_norm(self, xa, Rx, rows):
        rstd, Rm = self.norm_stats(xa, rows, D, [Rx])
        hb = self.hbr[self.hbi % 2]
        Rhb = self.R_hbr[self.hbi % 2]
        self.hbi += 1
        self.stt(hb[0:rows], xa, rstd, self.gbuf[0:rows], ALU.mult, ALU.mult, [Rx, Rm, self.R_gbuf], [Rhb])
        return hb, Rhb

    def _hT_tr(self, hbinfo, blk, rows, hT, R_h):
        hb, Rhb = hbinfo
        bk = self.banks(2)
        for half in range(2):
            pb = self.psb[bk[half]]
            for j in range(8):
                kc = half * 8 + j
                self.tr(pb[:, j * 128: j * 128 + rows], hb[0:rows, kc * 128:(kc + 1) * 128],
                        self.ident[0:rows, 0:rows], [Rhb], [self.R_ps[bk[half]]])
            srcv = pb[:, 0:1024].rearrange("p (a b) -> p a b", a=8)[:, :, 0:rows]
            dst = hT[:, half * 8:(half + 1) * 8, blk * rows:(blk + 1) * rows]
            self.copy("act" if half == 0 else "dve", dst, srcv, [self.R_ps[bk[half]]], [R_h])

    def _hT_block(self, xa, Rx, blk, rows, hT=None, R_h=None):
        if hT is None:
            hT, R_h = self.hT, self.R_hT[blk]
        self._hT_tr(self._hT_norm(xa, Rx, rows), blk, rows, hT, R_h)

    def q_tile(self, nblk, rows, xsrc, psrc, cosq, sinq, segs, ydst, mask_par):
        W = nblk * rows
        P = self.P
        w_in = self.w["w_in"]
        X = self.X
        R_X = self.R_X

        def win(c0):
            return lambda p0, n: w_in[:, c0 + p0: c0 + p0 + n].rearrange("(kc p) n -> p kc n", p=128)

        def wfull(w):
            return lambda p0, n: w[:, p0:p0 + n].rearrange("(kc p) n -> p kc n", p=128)

        self.dma("sp", self.gbuf, self.g["g_mix_pre"], [], [self.R_gbuf])
        for blk in range(nblk):
            self.dma("sp", X[0:rows, blk, :], xsrc[blk * rows:(blk + 1) * rows, :], [], [R_X[blk]])
        self.hT_all(nblk, rows)
        self.P.barrier()
        R_h = self.R_hT[0:nblk]

        def hrhs(kc):
            return self.hT[:, kc, 0:W]

        def hlhs(kc, blk):
            return self.hT[:, kc, blk * rows:(blk + 1) * rows]

        self.dma("sp", self.gq, self.gq_d, [], [self.R_gq])
        def cb_cq(p0, n, blk, ps, Rps):
            rstd, Rm = self.norm_stats(ps, rows, 512, [Rps])
            self.stt(self.k16[0:rows, 0:512], ps, rstd, self.gq[0:rows], ALU.mult, ALU.mult,
                     [Rps, Rm, self.R_gq], [self.R_k16])
            bk = self.banks(1)[0]
            pb = self.psb[bk]
            for j in range(4):
                self.tr(pb[:, j * 128: j * 128 + rows], self.k16[0:rows, j * 128:(j + 1) * 128],
                        self.ident[0:rows, 0:rows], [self.R_k16], [self.R_ps[bk]])
            srcv = pb[:, 0:512].rearrange("p (a b) -> p a b", a=4)[:, :, 0:rows]
            self.copy("act", self.cqT[:, :, blk * rows:(blk + 1) * rows], srcv, [self.R_ps[bk]], [self.R_cqT])
        self.lin_tm(win(3072), 16, 512, hlhs, R_h, nblk, rows, cb_cq)

        w_uq = self.w["w_uq"]
        uq4 = w_uq.rearrange("(kc p) (h e) -> p kc h e", p=128, e=192)

        def cb_qn(c, ps, Rps):
            qs = self.qstg[self.qi % 16]
            Rq = self.R_qstg[self.qi % 16]
            self.qi += 1
            self.ts("dve", qs[:, 0:W], ps, MLA_SCALE, None, ALU.mult, None, [Rps], [Rq])
            self.dma("sp", self.Qscr[2, c, :, 0:W], qs[:, 0:W], [Rq], [])
        self.lin_fm(lambda p0, n: uq4[:, :, p0 // 128:(p0 + n) // 128, 0:128], 4, 1024,
                    lambda kc: self.cqT[:, kc, 0:W], [self.R_cqT], W, cb_qn)

        def cb_qr(p0, n, blk, ps, Rps):
            self.dma("sp", self.tabc[0:rows, :], cosq[blk * rows:(blk + 1) * rows, :], [], [self.R_tab])
            self.dma("sp", self.tabs[0:rows, :], sinq[blk * rows:(blk + 1) * rows, :], [], [self.R_tab])
            self.rope(self.k16[0:rows, 0:512], ps, self.tabc[0:rows, :], self.tabs[0:rows, :], 8, rows,
                      [Rps, self.R_tab], self.R_k16, scale=MLA_SCALE)
            bk = self.banks(1)[0]
            pb = self.psb[bk]
            for j in range(8):
                self.tr(pb[0:64, j * 128: j * 128 + rows], self.k16[0:rows, j * 64:(j + 1) * 64],
                        self.ident[0:rows, 0:rows], [self.R_k16], [self.R_ps[bk]])
            srcv = pb[0:64, 0:1024].rearrange("p (a b) -> p a b", a=8)[:, :, 0:rows]
            self.copy("act", self.qrstg[0:64, :, blk * rows:(blk + 1) * rows], srcv, [self.R_ps[bk]],
                      [self.R_qrstg])
        self.lin_tm(lambda p0, n: uq4[:, :, :, 128:192], 4, 512,
                    lambda kc, blk: self.cqT[:, kc, blk * rows:(blk + 1) * rows], [self.R_cqT],
                    nblk, rows, cb_qr)
        self.dma("sp", self.Qscr[3, :, 0:64, 0:W].rearrange("h p t -> p h t"), self.qrstg[0:64, :, 0:W],
                 [self.R_qrstg], [])

        def cb_q(c, ps, Rps):
            qs = self.qstg[self.qi % 16]
            Rq = self.R_qstg[self.qi % 16]
            self.qi += 1
            self.ts("dve", qs[:, 0:W], ps, SB_SCALE, None, ALU.mult, None, [Rps], [Rq])
            self.dma("sp", self.Qscr[0, c, :, 0:W], qs[:, 0:W], [Rq], [])
            qs2 = self.qstg[self.qi % 16]
            Rq2 = self.R_qstg[self.qi % 16]
            self.qi += 1
            self.act(qs2[:, 0:W], ps, AF.Copy, [Rps], [Rq2], scale=-SB_SCALE)
            self.dma("sp", self.Qscr[1, c, :, 0:W], qs2[:, 0:W], [Rq2], [])
        self.lin_fm(win(0), 16, 1024, hrhs, R_h, W, cb_q)


        self.P.barrier()
        if mask_par is not None:
            self.dma("pool", self.mSB, self.masks["sb"][mask_par].rearrange("r p q -> p r q"), [], [self.R_mask["sb"]])
            self.dma("pool", self.mML, self.masks["ml"][mask_par].rearrange("r p q -> p r q"), [], [self.R_mask["ml"]])
        for seg in segs:
            self.attention(seg)

        if DEBUG and self.dbg_tile:
            self.dma("pool", self.dbgO[0], self.OsbT, [self.R_O], [], final=True)
            self.dma("pool", self.dbgO[1], self.OmlaT, [self.R_O], [], final=True)
        self.P.barrier()
        wb = self.w["w_branch"]
        for p0 in range(0, D, 512):
            def cb_g(which):
                def f(c, ps, Rps):
                    ci = c - p0 // 128
                    self.act(self.sgp[which][:, ci, 0:W], ps, AF.Sigmoid, [Rps], [self.R_sgp[which]])
                return f
            for which in range(2):
                c0 = 4160 + which * D + p0
                self.lin_fm(lambda q0, n, c0=c0: w_in[:, c0:c0 + n].rearrange("(kc p) n -> p kc n", p=128),
                            16, 512, hrhs, R_h, W, lambda c, ps, Rps, which=which:
                            self.act(self.sgp[which][:, c, 0:W], ps, AF.Sigmoid, [Rps], [self.R_sgp[which]]))

            def cb_b0(c, ps, Rps):
                self.tt("dve", self.tp[:, c, 0:W], ps, self.sgp[0][:, c, 0:W], ALU.mult,
                        [Rps, self.R_sgp[0]], [self.R_tp])
            self.lin_fm(lambda q0, n: wb[0, :, p0:p0 + n].rearrange("(kc p) n -> p kc n", p=128), 8, 512,
                        lambda kc: self.OsbT[:, kc, 0:W], [self.R_O], W, cb_b0)

            def cb_b1(c, ps, Rps):
                self.tt("dve", self.tp2[:, 0:W], ps, self.sgp[1][:, c, 0:W], ALU.mult,
                        [Rps, self.R_sgp[1]], [self.R_tp2])
                self.tt("dve", self.mergedT[:, p0 // 128 + c, 0:W], self.tp2[:, 0:W], self.tp[:, c, 0:W],
                        ALU.add, [self.R_tp2, self.R_tp], [self.R_merged])
            self.lin_fm(lambda q0, n: wb[1, :, p0:p0 + n].rearrange("(kc p) n -> p kc n", p=128), 8, 512,
                        lambda kc: self.OmlaT[:, kc, 0:W], [self.R_O], W, cb_b1)

        self.dma("sp", self.gbufB, self.g["g_mix_post"], [], [self.R_gbufB])

        def cb_store(p0, n, blk, ps, Rps):
            self.act(self.junk[0:rows, 0:n], ps, AF.Square, [Rps], [self.R_ssp[blk]],
                     accum=self.ssp[0:rows, blk * 4 + p0 // 512: blk * 4 + p0 // 512 + 1])
            self.tt("dve", self.stg[0:rows, blk, p0:p0 + n], ps, self.gbufB[0:rows, p0:p0 + n], ALU.mult,
                    [Rps, self.R_gbufB], [self.R_stg[blk]])
        self.lin_tm(wfull(self.w["w_out"]), 16, D, lambda kc, blk: self.mergedT[:, kc, blk * rows:(blk + 1) * rows],
                    [self.R_merged], nblk, rows, cb_store)
        if DEBUG and self.dbg_tile:
            self.dma("pool", self.dbgM, self.mergedT, [self.R_merged], [], final=True)
        self.norm_residual(nblk, rows)
        if DEBUG and self.dbg_tile:
            for blk in range(nblk):
                self.dma("sp", self.dbgX[0, blk * rows:(blk + 1) * rows, :], X[0:rows, blk, :], [R_X[blk]], [], final=True)

        self.dma("sp", self.gbuf, self.g["g_ffn_pre"], [], [self.R_gbuf])
        self.hT_all(nblk, rows)
        self.P.barrier()
        w_up = self.w["w_up"]
        w_down = self.w["w_down"]
        self.dma("sp", self.gbufB, self.g["g_ffn_post"], [], [self.R_gbufB])
        for q in range(4):
            def cb_up(c, ps, Rps):
                self.act(self.sq[:, 0:W], ps, AF.Square, [Rps], [self.R_sq])
                self.stt(self.uT[:, c, 0:W], ps, 0.0, self.sq[:, 0:W], ALU.is_gt, ALU.mult,
                         [Rps, self.R_sq], [self.R_uT])
            self.lin_fm(lambda p0, n, q=q: w_up[:, q * 2048 + p0: q * 2048 + p0 + n].rearrange(
                "(kc p) n -> p kc n", p=128), 16, 2048, hrhs, R_h, W, cb_up)

            def cb_dn(p0, n, blk, ps, Rps, q=q):
                if q == 0:
                    self.copy("act" if blk % 2 == 0 else "dve", self.stg[0:rows, blk, p0:p0 + n], ps, [Rps],
                              [self.R_stg[blk]])
                else:
                    sv = self.stg[0:rows, blk, p0:p0 + n]
                    self.tt("dve", sv, ps, sv, ALU.add, [Rps, self.R_stg[blk]], [self.R_stg[blk]])
                    if q == 3:
                        self.act(self.junk[0:rows, 0:n], sv, AF.Square, [self.R_stg[blk]], [self.R_ssp[blk]],
                                 accum=self.ssp[0:rows, blk * 4 + p0 // 512: blk * 4 + p0 // 512 + 1])
                        self.tt("dve", sv, sv, self.gbufB[0:rows, p0:p0 + n], ALU.mult,
                                [self.R_stg[blk], self.R_gbufB], [self.R_stg[blk]])
            self.lin_tm(lambda p0, n, q=q: w_down[q * 2048:(q + 1) * 2048, p0:p0 + n].rearrange(
                "(kc p) n -> p kc n", p=128), 16, D,
                lambda kc, blk: self.uT[:, kc, blk * rows:(blk + 1) * rows], [self.R_uT], nblk, rows, cb_dn)
        self.norm_residual(nblk, rows)
        if DEBUG and self.dbg_tile:
            for blk in range(nblk):
                self.dma("sp", self.dbgX[1, blk * rows:(blk + 1) * rows, :], X[0:rows, blk, :], [R_X[blk]], [], final=True)

        self.dma("sp", self.gbuf, self.g["g_ple_gate"], [], [self.R_gbuf])
        self.hT_all(nblk, rows)
        self.P.barrier()

        def cb_pg(p0, n, blk, ps, Rps):
            self.act(self.stg[0:rows, blk, p0:p0 + n], ps, AF.Sigmoid, [Rps], [self.R_stg[blk]])
        self.lin_tm(wfull(self.w["w_ple_gate"]), 16, D, hlhs, R_h, nblk, rows, cb_pg)
        for blk in range(nblk):
            self.dma("sp", self.pblk[0:rows, :], psrc[blk * rows:(blk + 1) * rows, :], [], [self.R_pblk])
            self.copy("dve", self.k16[0:rows, 0:256], self.pblk[0:rows, :], [self.R_pblk], [self.R_k16])
            bk = self.banks(1)[0]
            pb = self.psb[bk]
            for j in range(2):
                self.tr(pb[:, j * 128: j * 128 + rows], self.k16[0:rows, j * 128:(j + 1) * 128],
                        self.ident[0:rows, 0:rows], [self.R_k16], [self.R_ps[bk]])
            srcv = pb[:, 0:256].rearrange("p (a b) -> p a b", a=2)[:, :, 0:rows]
            self.copy("act", self.pT[:, :, blk * rows:(blk + 1) * rows], srcv, [self.R_ps[bk]], [self.R_pT])

        self.dma("sp", self.gbufB, self.g["g_ple_post"], [], [self.R_gbufB])

        def cb_pp(p0, n, blk, ps, Rps):
            sv = self.stg[0:rows, blk, p0:p0 + n]
            self.tt("dve", sv, ps, sv, ALU.mult, [Rps, self.R_stg[blk]], [self.R_stg[blk]])
            self.act(self.junk[0:rows, 0:n], sv, AF.Square, [self.R_stg[blk]], [self.R_ssp[blk]],
                     accum=self.ssp[0:rows, blk * 4 + p0 // 512: blk * 4 + p0 // 512 + 1])
            self.tt("dve", sv, sv, self.gbufB[0:rows, p0:p0 + n], ALU.mult,
                    [self.R_stg[blk], self.R_gbufB], [self.R_stg[blk]])
        self.lin_tm(wfull(self.w["w_ple"]), 2, D, lambda kc, blk: self.pT[:, kc, blk * rows:(blk + 1) * rows],
                    [self.R_pT], nblk, rows, cb_pp)
        self.norm_residual(nblk, rows)
        for blk in range(nblk):
            self.dma("sp", ydst[blk * rows:(blk + 1) * rows, :], X[0:rows, blk, :], [R_X[blk]], [], final=True)

    def hT_all(self, nblk, rows):
        st = [self.norm_stats(self.X[0:rows, blk, :], rows, D, [self.R_X[blk]]) for blk in range(nblk)]
        for blk in range(nblk):
            rstd, Rm = st[blk]
            hb = self.hbr[self.hbi % 2]
            Rhb = self.R_hbr[self.hbi % 2]
            self.hbi += 1
            self.stt(hb[0:rows], self.X[0:rows, blk, :], rstd, self.gbuf[0:rows], ALU.mult, ALU.mult,
                     [self.R_X[blk], Rm, self.R_gbuf], [Rhb])
            self._hT_tr((hb, Rhb), blk, rows, self.hT, self.R_hT[blk])

    def norm_residual(self, nblk, rows):
        st = []
        for blk in range(nblk):
            i = self.smi % 4
            self.smi += 1
            sm = self.smr[i]
            R = self.R_smr[i]
            self.P.add("dve", lambda e, sm=sm, blk=blk: e.reduce_sum(
                out=sm[0:rows, 0:1], in_=self.ssp[0:rows, blk * 4:(blk + 1) * 4], axis=mybir.AxisListType.X),
                [self.R_ssp[blk]], [R])
            st.append(self.rstd_tail(sm, R, rows, D))
        for blk in range(nblk):
            rstd, Rm = st[blk]
            self.stt(self.X[0:rows, blk, :], self.stg[0:rows, blk, :], rstd, self.X[0:rows, blk, :],
                     ALU.mult, ALU.add, [self.R_stg[blk], Rm, self.R_X[blk]], [self.R_X[blk]])

    def attention(self, seg):
        c0, nq, scr, blocks = seg["c0"], seg["nq"], seg["scr"], seg["blocks"]
        cols = slice(c0, c0 + nq)
        nb = len(blocks)
        CH = 8
        nvalid = sum(b[1] for b in blocks)
        self.dma("sp", self.KTr[0:64, 0:nvalid], scr["KTr"][:, 0:nvalid], [self.R_scr], [self.R_KTr])
        order = sorted(blocks, key=lambda b: -b[0])
        nk_of = {b[0]: b[1] for b in blocks}
        nch = (nb + 7) // 8
        sizes = [nb // nch + (1 if c < nb % nch else 0) for c in range(nch)]
        assert min(sizes) >= 2
        units = []
        chunks = []
        for h in range(NH):
            i = 0
            for c in range(nch):
                lo = order[i + sizes[c] - 1][0]
                chunks.append((h, lo, sizes[c], len(units)))
                for _ in range(sizes[c]):
                    kb, nk, mrel = order[i]
                    units.append((h, i, kb, nk, mrel, len(chunks) - 1))
                    i += 1
        first_of_head = {h: h * nb for h in range(NH)}

        def load_kv(g):
            h, lo, n, _ = chunks[g]
            slot = g % 2
            R = self.R_kv[slot]
            kv = self.kv[slot]
            nkl = nk_of[lo + n - 1]
            nf = n if nkl == 128 else n - 1
            ncols = nf * 128 + (0 if nkl == 128 else nkl)
            self.dma("sp", kv["KT"][:, 0:ncols], scr["KTsb"][h, :, lo * 128:lo * 128 + ncols], [self.R_scr], [R["KT"]])
            self.dma("sp", kv["KTn"][:, 0:ncols], scr["KTn"][h, :, lo * 128:lo * 128 + ncols], [self.R_scr], [R["KTn"]])
            if nf:
                self.dma("sp", kv["V"][:, 0:nf, :], scr["Vsb"][h, :, lo:lo + nf, :], [self.R_scr], [R["V"]])
                self.dma("sp", kv["Vm"][:, 0:nf, :], scr["Vm"][h, :, lo:lo + nf, :], [self.R_scr], [R["Vm"]])
            if nf < n:
                self.dma("sp", kv["V"][0:nkl, nf, :], scr["Vsb"][h, 0:nkl, lo + nf, :], [self.R_scr], [R["V"]])
                self.dma("sp", kv["Vm"][0:nkl, nf, :], scr["Vm"][h, 0:nkl, lo + nf, :], [self.R_scr], [R["Vm"]])

        def load_q(h):
            qb = self.qh[h % 2]
            R = self.R_qh[h % 2]
            for t in range(3):
                self.dma("sp", qb[t][:, 0:nq], self.Qscr[t, h, :, cols], [self.R_Qscr], [R[t]])
            self.dma("sp", qb[3][0:64, 0:nq], self.Qscr[3, h, 0:64, cols], [self.R_Qscr], [R[3]])

        def prefetch(step):
            for g in range(len(chunks)):
                if chunks[g][3] + 1 == step and g + 1 < len(chunks):
                    load_kv(g + 1)
            for h in range(NH):
                if first_of_head[h] + 1 == step and h + 1 < NH:
                    load_q(h + 1)

        def ensure_kv(u):
            g = units[u][5]
            h, lo, n, _ = chunks[g]
            return self.kv[g % 2], self.R_kv[g % 2], lo

        def ensure_q(h):
            return self.qh[h % 2], self.R_qh[h % 2]

        load_kv(0)
        load_q(0)

        st = {}
        ones = self.ones
        tri = self.tri
        idn = self.ident
        nid = self.negid
        PZ, PG, PS_, POS, POM, PDEN = 0, (1, 2), (3, 4), 5, 6, 7

        def s1(u):
            h, i, kb, nk, mrel, _g = units[u]
            kv, Rkv, b0 = ensure_kv(u)
            qb, Rq = ensure_q(h)
            lb = kb - b0
            d = dict(kv=kv, Rkv=Rkv, lb=lb, qb=qb, Rq=Rq)
            st[u] = d
            z = self.ps[PZ][0:nk, 0:nq]
            Rz = self.R_ps[PZ]
            kt = kv["KT"][:, lb * 128: lb * 128 + nk]
            Rmsb = self.R_mask["S" if mrel == "S" else "sb"]
            self.mm(z, kt, qb[0][:, 0:nq], True, mrel is None, [Rkv["KT"], Rq[0]], [Rz])
            if mrel is not None:
                self.mm(z, nid[0:nk, 0:nk], self.mask_ap("sb", mrel, nk, nq), False, True, [Rmsb], [Rz])
            sb_ = PS_[u % 2]
            s = self.ps[sb_][0:nk, 0:nq]
            Rs = self.R_ps[sb_]
            self.mm(s, kv["KTn"][:, lb * 128: lb * 128 + nk], qb[2][:, 0:nq], True, False, [Rkv["KTn"], Rq[2]], [Rs])
            self.mm(s, self.KTr[0:64, kb * 128: kb * 128 + nk], qb[3][0:64, 0:nq], False, bool(mrel is None or seg.get("nomla")),
                    [self.R_KTr, Rq[3]], [Rs])
            if mrel is not None and not seg.get("nomla"):
                self.mm(s, nid[0:nk, 0:nk], self.mask_ap("ml", mrel, nk, nq), False, True, [self.R_mask["ml"]], [Rs])

        def s2(u):
            h, i, kb, nk, mrel, _g = units[u]
            z = self.ps[PZ][0:nk, 0:nq]
            e1 = self.e1[u % 2][0:nk, 0:nq]
            Lp = self.Lp[u % 2][0:nk, 0:nq]
            self.act(e1, z, AF.Exp, [self.R_ps[PZ]], [self.R_e1[u % 2]])
            self.act(Lp, e1, AF.Ln, [self.R_e1[u % 2]], [self.R_Lp[u % 2]], bias=1.0)
            sb_ = PS_[u % 2]
            self.act(self.Pm[u % 2][0:nk, 0:nq], self.ps[sb_][0:nk, 0:nq], AF.Exp, [self.R_ps[sb_]],
                     [self.R_Pm[u % 2]])

        def s3(u):
            h, i, kb, nk, mrel, _g = units[u]
            d = st[u]
            kv, Rkv, lb, qb, Rq = d["kv"], d["Rkv"], d["lb"], d["qb"], d["Rq"]
            gb = PG[u % 2]
            g = self.ps[gb][0:nk, 0:nq]
            Rg = self.R_ps[gb]
            Lp = self.Lp[u % 2][0:nk, 0:nq]
            self.mm(g, tri[0:nk, 0:nk], Lp, True, False, [self.R_Lp[u % 2]], [Rg])
            if i > 0:
                self.mm(g, ones[:, 0:nk], self.Lsum[:, 0:nq], False, False, [self.R_Lsum], [Rg])
            kt = kv["KT"][:, lb * 128: lb * 128 + nk]
            self.mm(g, kt, qb[1][:, 0:nq], False, mrel is None, [Rkv["KT"], Rq[1]], [Rg])
            if mrel is not None:
                self.mm(g, idn[0:nk, 0:nk], self.mask_ap("sb", mrel, nk, nq), False, True,
                        [self.R_mask["S" if mrel == "S" else "sb"]], [Rg])
            if i == 0 and nb > 1:
                self.P.add("pool", lambda e: e.memset(self.Lsum[:, 0:nq], 0.0), [], [self.R_Lsum])
            if i < nb - 1:
                self.tt("dve", self.Lsum[0:nk, 0:nq], self.Lsum[0:nk, 0:nq], Lp, ALU.add,
                        [self.R_Lsum, self.R_Lp[u % 2]], [self.R_Lsum])
            Pm = self.Pm[u % 2][0:nk, 0:nq]
            if DEBUG and self.dbg_tile and h == 0 and nq == 512:
                self.dma("pool", self.dbgP[kb], Pm, [self.R_Pm[u % 2]], [], final=True)
            self.mm(self.ps[POM][:, 0:nq], kv["Vm"][0:nk, lb, :], Pm, i == 0, i == nb - 1,
                    [Rkv["Vm"], self.R_Pm[u % 2]], [self.R_ps[POM]])
            self.mm(self.ps[PDEN][:, 0:nq], ones[0:nk, :], Pm, i == 0, i == nb - 1,
                    [self.R_Pm[u % 2]], [self.R_ps[PDEN]])
            if i == nb - 1:
                self.P.add("dve", lambda e: e.reciprocal(self.rden[:, 0:nq], self.ps[PDEN][:, 0:nq]),
                           [self.R_ps[PDEN]], [self.R_rden])
                self.tt("dve", self.OmlaT[:, h, cols], self.ps[POM][:, 0:nq], self.rden[:, 0:nq], ALU.mult,
                        [self.R_ps[POM], self.R_rden], [self.R_O])

        def s4(u):
            h, i, kb, nk, mrel, _g = units[u]
            gb = PG[u % 2]
            self.act(self.Wt[u % 2][0:nk, 0:nq], self.ps[gb][0:nk, 0:nq], AF.Exp, [self.R_ps[gb]],
                     [self.R_Wt[u % 2]], scale=-1.0)

        def s5(u):
            h, i, kb, nk, mrel, _g = units[u]
            d = st.pop(u)
            kv, Rkv, lb = d["kv"], d["Rkv"], d["lb"]
            if DEBUG and self.dbg_tile and h == 0 and nq == 512:
                self.dma("pool", self.dbgW[kb], self.Wt[u % 2][0:nk, 0:nq], [self.R_Wt[u % 2]], [], final=True)
            self.mm(self.ps[POS][:, 0:nq], kv["V"][0:nk, lb, :], self.Wt[u % 2][0:nk, 0:nq], i == 0, i == nb - 1,
                    [Rkv["V"], self.R_Wt[u % 2]], [self.R_ps[POS]])
            if i == nb - 1:
                self.copy("act", self.OsbT[:, h, cols], self.ps[POS][:, 0:nq], [self.R_ps[POS]], [self.R_O])

        n = len(units)
        for step in range(n + 2):
            if step < n:
                s1(step)
                s2(step)
            if 0 <= step - 1 < n:
                s3(step - 1)
                s4(step - 1)
            if 0 <= step - 2 < n:
                s5(step - 2)
            prefetch(step)

    def mask_ap(self, kind, mrel, nk, nq):
        if mrel == "S":
            return self.mS[0:nk, 0:nq]
        m = self.mSB if kind == "sb" else self.mML
        return m[0:nk, mrel, 0:nq]

    def build(self):
        nc = self.nc
        self.xfull = self.din("xfull", [SEQ, D])
        self.xown = self.din("xown", [2048, D])
        self.pown = self.din("pown", [2048, 256])
        self.xs = self.din("xs", [64, D])
        self.pss = self.din("pss", [64, 256])
        self.csk = self.din("csk", [2, PAST, 1024])
        self.csv = self.din("csv", [2, PAST, 1024])
        self.cckv = self.din("cckv", [2, PAST, 512])
        self.ckr = self.din("ckr", [2, PAST, 64])
        self.cosk = self.din("cosk", [SEQ, 64])
        self.sink = self.din("sink", [SEQ, 64])
        self.cosq = self.din("cosq", [2048, 512])
        self.sinq = self.din("sinq", [2048, 512])
        self.cosks = self.din("cosks", [64, 64])
        self.sinks = self.din("sinks", [64, 64])
        self.cosqs = self.din("cosqs", [64, 512])
        self.sinqs = self.din("sinqs", [64, 512])
        self.masks = {"sb": self.din("masksb", [2, 8, 128, 512]), "ml": self.din("maskml", [2, 8, 128, 512])}
        self.maskS_d = self.din("masks_s", [128, 64])
        self.consts_d = self.din("consts", [128, 512])
        self.w = {}
        for name, shp in (("w_in", [D, 8256]), ("w_uq", [512, 1536]), ("w_uk", [512, 1024]),
                          ("w_uv", [512, 1024]), ("w_branch", [2, 1024, D]), ("w_out", [D, D]),
                          ("w_up", [D, 8192]), ("w_down", [8192, D]), ("w_ple_gate", [D, D]),
                          ("w_ple", [256, D])):
            self.w[name] = self.din(name, shp)
        self.g = {}
        for name in ("g_mix_pre", "g_mix_post", "g_ffn_pre", "g_ffn_post", "g_ple_gate", "g_ple_post"):
            self.g[name] = self.din(name, [128, D])
        self.gq_d = self.din("g_q", [128, 512])
        self.gkv_d = self.din("g_kv", [128, 512])
        self.y_own = self.dout("y_own", [2048, D])
        self.y_s = self.dout("y_s", [64, D])
        self.o_full = {"nk": self.dout("nk_full", [SEQ, 1024]), "nv": self.dout("nv_full", [SEQ, 1024]),
                       "nckv": self.dout("nckv_full", [SEQ, 512]), "nkr": self.dout("nkr_full", [SEQ, 64])}
        self.o_s = {"nk": self.dout("nk_s", [64, 1024]), "nv": self.dout("nv_s", [64, 1024]),
                    "nckv": self.dout("nckv_s", [64, 512]), "nkr": self.dout("nkr_s", [64, 64])}
        ds = (lambda n, sh: self.dout(n, sh, BF16)) if DEBUG else self.dscr
        self.scrP = {"KTsb": ds("KTsb", [NH, 128, SEQ]), "Vsb": ds("Vsb", [NH, 128, 32, 128]),
                     "KTn": ds("KTn", [NH, 128, SEQ]), "Vm": ds("Vm", [NH, 128, 32, 128]),
                     "KTr": ds("KTr", [64, SEQ])}
        if DEBUG:
            self.dbgP = self.dout("dbgP", [8, 128, NT])
            self.dbgW = self.dout("dbgW", [8, 128, NT])
        self.scrS = []
        for i in range(2):
            self.scrS.append({"KTsb": self.dscr("KTsb_s%d" % i, [NH, 128, SKEYS]),
                              "Vsb": self.dscr("Vsb_s%d" % i, [NH, 128, 9, 128]),
                              "KTn": self.dscr("KTn_s%d" % i, [NH, 128, SKEYS]),
                              "Vm": self.dscr("Vm_s%d" % i, [NH, 128, 9, 128]),
                              "KTr": self.dscr("KTr_s%d" % i, [64, SKEYS])})
        if DEBUG:
            self.Qscr = self.dout("Qscr", [4, NH, 128, NT], BF16)
            self.dbgO = self.dout("dbgO", [2, 128, NH, NT])
            self.dbgM = self.dout("dbgM", [128, 16, NT])
            self.dbgX = self.dout("dbgX", [2, NT, D])
        else:
            self.Qscr = self.dscr("Qscr", [4, NH, 128, NT])

        with ExitStack() as es:
            KB = 1024
            self.arena_bytes = 200 * KB
            self.arena = es.enter_context(nc.sbuf_tensor("arena", [128, self.arena_bytes // 2], BF16))
            self.ps = [es.enter_context(nc.psum_tensor("ps%d" % i, [128, 512], F32)) for i in range(8)]
            self.psb = [p[:].bitcast(BF16) for p in self.ps]
            self.ps = [p[:] for p in self.ps]
            self.R_ps = [Res("ps%d" % i, excl=True) for i in range(8)]
            self.layout()
            self.P.planning = True
            self.program()
            self.P.planning = False
            self.wconv = [self.dscr("wc%d" % k, [128, kcn * ncols]) for k, (_, kcn, ncols) in enumerate(self.wspecs)]
            self.R_wconv = [Res("wc%d" % k) for k in range(len(self.wspecs))]
            self.R_cwin = [Res("cwin%d" % k) for k in range(4)]
            self.wi = 0
            self.wissued = 0
            self.bank_rr = 0
            self.oi = self.qi = self.kvslot = self.qslot = 0
            self.program()
            self.P.emit(nc, self.finals)
        return nc

    def layout(self):
        KB = 1024
        v = self.view
        o = 0
        cst = v(o, [512], BF16); o += 1 * KB
        self.ident = cst[:, 0:128]
        self.tri = cst[:, 128:256]
        self.ones = cst[:, 256:384]
        self.negid = cst[:, 384:512]
        self.R_cst = Res("cst")
        smv = v(o, [64], F32); o += 256
        self.smr = [smv[:, 2 * i:2 * i + 2] for i in range(4)]
        self.R_smr = [Res("sm%d" % i) for i in range(4)]
        self.mS = v(o, [64], BF16); o += 128
        self.hT = v(o, [16, 512], BF16); o += 16 * KB
        self.R_hT = [Res("hT%d" % i) for i in range(4)]
        self.wt_view = [v(o + i * 16 * KB, [16, 512], BF16) for i in range(2)]; o += 32 * KB
        self.R_wt = [Res("wt0"), Res("wt1")]
        self.gbuf = v(o, [2048], F32); o += 8 * KB
        self.R_gbuf = Res("gbuf")
        self.gbufB = v(o, [2048], F32); o += 8 * KB
        self.R_gbufB = Res("gbufB")
        self.ssp = smv[:, 16:32]
        self.R_ssp = [Res("ssp%d" % i) for i in range(4)]
        self.hbr = [v(o + i * 4 * KB, [2048], BF16) for i in range(2)]; o += 8 * KB
        self.R_hbr = [Res("hb0"), Res("hb1")]
        self.junk = v(o, [2048], BF16); o += 4 * KB
        self.k16 = v(o, [1024], BF16); o += 2 * KB
        self.R_k16 = Res("k16")
        base = o
        self.R_tab = Res("tab")
        self.R_rt = Res("rt")
        self.R_gq = Res("gq")
        self.xblk = [v(o + i * 8 * KB, [2048], F32) for i in range(2)]; o += 16 * KB
        self.R_xblk = [Res("xb0"), Res("xb1")]
        self.ostg = [v(o + i * 2 * KB, [512], F32) for i in range(6)]; o += 12 * KB
        self.R_ostg = [Res("os%d" % i) for i in range(6)]
        self.k16b = [v(o + i * 2 * KB, [1024], BF16) for i in range(2)]; o += 4 * KB
        self.R_k16b = [Res("k16b0"), Res("k16b1")]
        self.k16r = [v(o + i * 2 * KB, [1024], BF16) for i in range(2)]; o += 4 * KB
        self.R_k16r = [Res("k16r0"), Res("k16r1")]
        self.hTA = [self.hT, v(o, [16, 512], BF16)]; o += 16 * KB
        self.R_hTA = [self.R_hT, [Res("hTB%d" % i) for i in range(4)]]
        self.r16 = v(o, [64], BF16); o += 128
        self.R_r16 = Res("r16")
        self.KTstg = v(o, [8, 512], BF16); o += 8 * KB
        self.KTnstg = v(o, [8, 512], BF16); o += 8 * KB
        self.Vstg = v(o, [8, 4, 128], BF16); o += 8 * KB
        self.Vmstg = v(o, [8, 4, 128], BF16); o += 8 * KB
        self.KTrstg = v(o, [1, 512], BF16); o += 1 * KB
        self.ckvT = v(o, [4, 512], BF16); o += 4 * KB
        self.gkv = v(o, [512], F32); o += 2 * KB
        self.A_tab = (v(o, [512], F32), v(o + 2 * KB, [512], F32), v(o + 4 * KB, [512], F32),
                      v(o + 6 * KB, [512], F32)); o += 8 * KB
        self.wr_uk = v(o, [4, 1024], BF16); o += 8 * KB
        self.wr_uv = v(o, [4, 1024], BF16); o += 8 * KB
        self.wr_kr = v(o, [16, 64], BF16); o += 2 * KB
        self.R_wres = Res("wres")
        self.R_KTstg = [[Res() for _ in range(4)] for _ in range(2)]
        self.R_Vstg = [[Res() for _ in range(4)] for _ in range(2)]
        self.R_Vmstg = [[Res() for _ in range(4)] for _ in range(2)]
        self.R_KTnstg = [Res() for _ in range(8)]
        self.R_KTrstg = [Res() for _ in range(4)]
        self.R_ckvT = [Res() for _ in range(4)]
        endA = o
        o = base
        self.X = v(o, [4, 2048], F32); o += 32 * KB
        self.R_X = [Res("X%d" % i) for i in range(4)]
        self.OsbT = v(o, [8, 512], BF16); o += 8 * KB
        self.OmlaT = v(o, [8, 512], BF16); o += 8 * KB
        self.R_O = Res("O")
        ub = o
        self.cqT = v(o, [4, 512], BF16); o += 4 * KB
        self.R_cqT = Res("cqT")
        self.qstg = [v(o + i * KB, [512], BF16) for i in range(16)]; o += 16 * KB
        self.R_qstg = [Res() for _ in range(16)]
        self.qrstg = v(o, [8, 512], BF16); o += 8 * KB
        self.R_qrstg = Res()
        self.gq = v(o, [512], F32); o += 2 * KB
        self.B_tab = (v(o, [512], F32), v(o + 2 * KB, [512], F32), v(o + 4 * KB, [512], F32),
                      v(o + 6 * KB, [512], F32)); o += 8 * KB
        end2 = o
        o = ub
        self.qh = []
        for s in range(2):
            self.qh.append([v(o + t * KB, [512], BF16) for t in range(4)]); o += 4 * KB
        self.R_qh = [[Res() for _ in range(4)] for _ in range(2)]
        self.kv = []
        for s in range(2):
            self.kv.append({"KT": v(o, [1024], BF16), "KTn": v(o + 2 * KB, [1024], BF16),
                            "V": v(o + 4 * KB, [8, 128], BF16), "Vm": v(o + 6 * KB, [8, 128], BF16)})
            o += 8 * KB
        self.R_kv = [{k: Res() for k in ("KT", "KTn", "V", "Vm")} for _ in range(2)]
        self.KTr = v(o, [SEQ], BF16); o += 8 * KB
        self.R_KTr = Res()
        self.mSB = v(o, [8, 512], BF16); o += 8 * KB
        self.mML = v(o, [8, 512], BF16); o += 8 * KB
        self.R_mask = {"sb": Res(), "ml": Res(), "S": Res()}
        self.e1 = [v(o + i * 2 * KB, [512], F32) for i in range(2)]; o += 4 * KB
        self.Lp = [v(o + i * KB, [512], BF16) for i in range(2)]; o += 2 * KB
        self.Wt = [v(o + i * KB, [512], BF16) for i in range(2)]; o += 2 * KB
        self.Pm = [v(o + i * KB, [512], BF16) for i in range(2)]; o += 2 * KB
        self.Lsum = v(o, [512], BF16); o += 1 * KB
        self.rden = v(o, [512], F32); o += 2 * KB
        self.R_e1, self.R_Lp, self.R_Wt, self.R_Pm = ([Res(), Res()] for _ in range(4))
        self.R_Lsum, self.R_rden = Res(), Res()
        end3 = o
        o = ub
        self.stg = v(o, [4, 2048], F32); o += 32 * KB
        self.R_stg = [Res() for _ in range(4)]
        so = o
        self.mergedT = v(o, [16, 512], BF16); o += 16 * KB
        self.R_merged = Res()
        self.sgp = [v(o + i * 4 * KB, [4, 512], BF16) for i in range(2)]; o += 8 * KB
        self.R_sgp = [Res(), Res()]
        self.tp = v(o, [4, 512], F32); o += 8 * KB
        self.R_tp = Res()
        self.tp2 = v(o, [512], F32); o += 2 * KB
        self.R_tp2 = Res()
        end4 = o
        o = so
        self.uT = v(o, [16, 512], BF16); o += 16 * KB
        self.R_uT = Res()
        self.sq = v(o, [512], F32); o += 2 * KB
        self.R_sq = Res()
        o = so
        self.pblk = v(o, [256], F32); o += 1 * KB
        self.R_pblk = Res("pblk")
        self.pT = v(o, [2, 512], BF16); o += 2 * KB
        self.R_pT = Res("pT")
        self.R_scr = Res("scr")
        self.R_Qscr = Res("Qscr")
        self.tabset = None
        assert max(endA, end2, end3, end4) <= self.arena_bytes, (endA, end2, end3, end4)

    def use_tabs(self, t):
        self.tabc, self.tabs, self.rt1, self.rt2 = t

    def program(self):
        self.oi = self.qi = self.kvslot = self.qslot = 0
        self.smi = self.hbi = self.k16i = self.xbi = 0
        self.deferred = []
        self.use_tabs(self.A_tab)
        self.dma("pool", self.arena[:, 0:512], self.consts_d, [], [self.R_cst])
        self.dma("pool", self.mS, self.maskS_d, [], [self.R_cst])
        self.dma("sp", self.gkv, self.gkv_d, [], [self.R_cst])
        self.P.barrier()
        self.wq = "sp"
        self.wdone = set()
        self.wuses = {}
        self.wb_pending = None
        self.dma("sp", self.gbuf, self.g["g_mix_pre"], [], [self.R_gbuf])
        nA = 8 if STAGE != "A0" else 1

        def srcP(t):
            return {"cos": self.cosk[t * 512:(t + 1) * 512, :], "sin": self.sink[t * 512:(t + 1) * 512, :]}

        def outP(t):
            return {k: a[t * 512:(t + 1) * 512, :] for k, a in self.o_full.items()}
        w_in = self.w["w_in"]
        self.dma("pool", self.wr_uk, self.w["w_uk"].rearrange("(kc p) n -> p kc n", p=128), [], [self.R_wres])
        self.dma("pool", self.wr_uv, self.w["w_uv"].rearrange("(kc p) n -> p kc n", p=128), [], [self.R_wres])
        self.dma("pool", self.wr_kr, w_in[:, 4096:4160].rearrange("(kc p) n -> p kc n", p=128), [], [self.R_wres])
        self.kv_hT(self.xfull[0:512, :], 4, 128, 0)
        for t in range(nA):
            panels = self.kv_panels(t % 2, 4, 128, srcP(t), self.scrP, t * 512, outP(t))
            nxt = t + 1 < nA
            xn = self.xfull[(t + 1) * 512:(t + 2) * 512, :] if nxt else None
            for p in range(5):
                if nxt and p < 4:
                    info = self.kv_hT_norm(xn, p, 128)
                panels[p]()
                if nxt and p < 4:
                    self._hT_tr(info, p, 128, self.hTA[(t + 1) % 2], self.R_hTA[(t + 1) % 2][p])
            panels[5]()
        self.flush()
        if STAGE == "A0":
            return
        for i in range(2):
            for t in range(2):
                self.kv_cache_tile({"k": self.csk[i], "v": self.csv[i], "ckv": self.cckv[i], "kr": self.ckr[i]},
                                   self.scrS[i], t * 512)
            srcS = {"cos": self.cosks[i * 32:(i + 1) * 32, :], "sin": self.sinks[i * 32:(i + 1) * 32, :]}
            outS = {k: a[i * 32:(i + 1) * 32, :] for k, a in self.o_s.items()}
            self.kv_hT(self.xs[i * 32:(i + 1) * 32, :], 1, 32, 0)
            self.kv_proj(1, 0, 1, 32, srcS, self.scrS[i], PAST, outS)
            self.kv_proj(2, 0, 1, 32, srcS, self.scrS[i], PAST, outS)
            self.flush()
        self.P.barrier()
        if STAGE == "A":
            return
        self.use_tabs(self.B_tab)
        self.wq = "pool"
        for j in range(4 if STAGE != "B0" else 1):
            nblk = 4 * TMAX[j] + 4
            blocks = []
            for kb in range(nblk):
                rel = kb - (nblk - 8)
                blocks.append((kb, 128, rel if rel >= 0 else None))
            self.dbg_tile = (j == 0)
            self.q_tile(4, 128, self.xown[j * 512:(j + 1) * 512, :], self.pown[j * 512:(j + 1) * 512, :],
                        self.cosq[j * 512:(j + 1) * 512, :], self.sinq[j * 512:(j + 1) * 512, :],
                        [dict(c0=0, nq=512, scr=self.scrP, blocks=blocks)],
                        self.y_own[j * 512:(j + 1) * 512, :], j % 2)
        if STAGE == "B0":
            return
        segs = []
        for i in range(2):
            blocks = [(kb, 128, None) for kb in range(8)] + [(8, 32, "S")]
            segs.append(dict(c0=i * 32, nq=32, scr=self.scrS[i], blocks=blocks, nomla=True))
        self.q_tile(2, 32, self.xs, self.pss, self.cosqs, self.sinqs, segs, self.y_s, None)


_NC = None
STAGE = "all"
DEBUG = False


def _rope_tables(pos):
    half = 32
    freqs = (10000.0 ** (-np.arange(half, dtype=np.float32) / half)).astype(np.float32)
    ang = pos.astype(np.float32)[:, None] * freqs[None, :]
    c = np.cos(ang).astype(np.float32)
    s = np.sin(ang).astype(np.float32)
    return np.concatenate([c, c], 1), np.concatenate([-s, s], 1)


def _diag_mask(i, kind):
    m = np.zeros((128, 512), np.float32)
    s = np.arange(128)[:, None]
    for qb in range(4):
        t = np.arange(128)[None, :]
        if qb < i:
            blk = np.full((128, 128), MASKV, np.float32)
        elif qb > i:
            blk = np.zeros((128, 128), np.float32)
        else:
            if kind == "sb":
                vis = s < t
            else:
                vis = (s // 64) <= (t // 64)
            blk = np.where(vis, 0.0, MASKV).astype(np.float32)
        m[:, qb * 128:(qb + 1) * 128] = blk
    return m


def _mask_sets(kind):
    mx = np.zeros((8, 128, 512), np.float32)
    mn = np.zeros((8, 128, 512), np.float32)
    for i in range(4):
        mx[4 + i] = _diag_mask(i, kind)
        mn[i] = _diag_mask(i, kind)
        mn[4 + i] = MASKV
    return mx, mn


def kernel(**inp):
    global _NC
    f = lambda a: np.ascontiguousarray(np.asarray(a, dtype=np.float32))
    xp = f(inp["x_prompt"]); xsmp = f(inp["x_sample"])
    csk = f(inp["cache_sb_k"])[0].reshape(16, PAST, 1024)
    csv = f(inp["cache_sb_v"])[0].reshape(16, PAST, 1024)
    cckv = f(inp["cache_mla_ckv"])[0]
    ckr = f(inp["cache_mla_krope"])[0]
    pp = f(inp["p_prompt"])[0]; psm = f(inp["p_sample"])[0]
    if _NC is None:
        _NC = Builder().build()
    nc = _NC
    shared = {}
    for name in ("w_in", "w_uq", "w_uk", "w_uv", "w_branch", "w_out", "w_up", "w_down", "w_ple_gate", "w_ple"):
        shared[name] = f(inp[name])[0]
    for name in ("g_mix_pre", "g_mix_post", "g_ffn_pre", "g_ffn_post", "g_ple_gate", "g_ple_post"):
        shared[name] = np.ascontiguousarray(np.broadcast_to(f(inp[name])[0][None, :], (128, D)))
    shared["g_q"] = np.ascontiguousarray(np.broadcast_to(f(inp["g_q"])[0][None, :], (128, 512)))
    shared["g_kv"] = np.ascontiguousarray(np.broadcast_to(f(inp["g_kv"])[0][None, :], (128, 512)))
    ck, sk = _rope_tables(np.arange(SEQ))
    shared["cosk"], shared["sink"] = ck, sk
    cks, sks = _rope_tables(PAST + np.arange(DSEQ))
    shared["cosks"] = np.ascontiguousarray(np.tile(cks, (2, 1)))
    shared["sinks"] = np.ascontiguousarray(np.tile(sks, (2, 1)))
    shared["cosqs"] = np.ascontiguousarray(np.tile(np.tile(cks, (1, 8)), (2, 1)))
    shared["sinqs"] = np.ascontiguousarray(np.tile(np.tile(sks, (1, 8)), (2, 1)))
    eye = np.eye(128, dtype=np.float32)
    tri = (np.arange(128)[:, None] >= np.arange(128)[None, :]).astype(np.float32)
    shared["consts"] = np.ascontiguousarray(np.concatenate([eye, tri, np.ones((128, 128), np.float32), -eye], 1))
    mS = np.zeros((128, 64), np.float32)
    mS[0:32, 0:32] = np.where(np.arange(32)[:, None] < np.arange(32)[None, :], 0.0, MASKV)
    shared["masks_s"] = mS
    msets = {k: _mask_sets(k) for k in ("sb", "ml")}
    in_maps = []
    for c in range(8):
        b, r = c // 2, c % 2
        own = OWN[r]
        m = dict(shared)
        m["xfull"] = xp[b]
        m["xown"] = np.ascontiguousarray(np.concatenate([xp[b, t * 512:(t + 1) * 512] for t in own], 0))
        m["pown"] = np.ascontiguousarray(np.concatenate([pp[b, t * 512:(t + 1) * 512] for t in own], 0))
        m["xs"] = np.ascontiguousarray(xsmp[2 * c:2 * c + 2].reshape(64, D))
        m["pss"] = np.ascontiguousarray(psm[2 * c:2 * c + 2].reshape(64, 256))
        m["csk"] = np.ascontiguousarray(csk[2 * c:2 * c + 2])
        m["csv"] = np.ascontiguousarray(csv[2 * c:2 * c + 2])
        m["cckv"] = np.ascontiguousarray(cckv[2 * c:2 * c + 2])
        m["ckr"] = np.ascontiguousarray(ckr[2 * c:2 * c + 2])
        pos = np.concatenate([np.arange(t * 512, (t + 1) * 512) for t in own])
        cq, sq = _rope_tables(pos)
        m["cosq"] = np.ascontiguousarray(np.tile(cq, (1, 8)))
        m["sinq"] = np.ascontiguousarray(np.tile(sq, (1, 8)))
        for k in ("sb", "ml"):
            mx, mn = msets[k]
            par0 = mn if r == 0 else mx
            par1 = mx if r == 0 else mn
            m["mask" + k] = np.ascontiguousarray(np.stack([par0, par1], 0))
        in_maps.append(m)
    res = run_bass_kernel_spmd(nc, in_maps, core_ids=list(range(8)))
    R = res.results
    if DEBUG:
        global _DBG
        _DBG = R
    return _post(R)


def _post(R):
    yp = np.zeros((4, SEQ, D), np.float32)
    ys = np.zeros((16, DSEQ, D), np.float32)
    nkp = np.zeros((1, 4, SEQ, 8, 128), np.float32); nvp = np.zeros_like(nkp)
    nckvp = np.zeros((1, 4, SEQ, 512), np.float32); nkrp = np.zeros((1, 4, SEQ, 64), np.float32)
    nks = np.zeros((1, 16, DSEQ, 8, 128), np.float32); nvs = np.zeros_like(nks)
    nckvs = np.zeros((1, 16, DSEQ, 512), np.float32); nkrs = np.zeros((1, 16, DSEQ, 64), np.float32)
    for c in range(8):
        b, r = c // 2, c % 2
        for j, t in enumerate(OWN[r]):
            sl = slice(t * 512, (t + 1) * 512)
            yp[b, sl] = R[c]["y_own"][j * 512:(j + 1) * 512]
            nkp[0, b, sl] = R[c]["nk_full"][sl].reshape(512, 8, 128)
            nvp[0, b, sl] = R[c]["nv_full"][sl].reshape(512, 8, 128)
            nckvp[0, b, sl] = R[c]["nckv_full"][sl]
            nkrp[0, b, sl] = R[c]["nkr_full"][sl]
        ys[2 * c:2 * c + 2] = R[c]["y_s"].reshape(2, DSEQ, D)
        nks[0, 2 * c:2 * c + 2] = R[c]["nk_s"].reshape(2, DSEQ, 8, 128)
        nvs[0, 2 * c:2 * c + 2] = R[c]["nv_s"].reshape(2, DSEQ, 8, 128)
        nckvs[0, 2 * c:2 * c + 2] = R[c]["nckv_s"].reshape(2, DSEQ, 512)
        nkrs[0, 2 * c:2 * c + 2] = R[c]["nkr_s"].reshape(2, DSEQ, 64)
    return (yp, ys, nkp, nvp, nckvp, nkrp, nks, nvs, nckvs, nkrs)
```
